# Optimizing a Trainium2 kernel written in Bass

```python
import jax, jax.numpy as jnp
from jax import lax
import numpy as np

D_MODEL = 1024
BATCH = 8
SEQ = 2048
DEPTH = 2

D_MIX = D_MODEL
GROUP_W = D_MIX // 4
GRID_W = 64
NA_HEADS = 8
NA_HEAD_DIM = GROUP_W // NA_HEADS
NA_ROWS = 8
NA_COLS = 16
CONV_WIDTH = 31
POOL_WINDOWS = (2, 4, 8, 16)
POOL_GROUPS = len(POOL_WINDOWS)
POOL_GROUP_W = GROUP_W // POOL_GROUPS
SGU_HEADS = 4
SGU_HEAD_DIM = GROUP_W // SGU_HEADS
SGU_CHUNK = 128
D_IN = 8 * GROUP_W + D_MIX
RMS_EPS = 1e-6
LN_EPS = 1e-5

kernel_name = "hybrid_parallel_na_conv_pool_sgu_encoder"


def rmsnorm(x, g):
    xf = x.astype(jnp.float32)
    y = xf * lax.rsqrt(jnp.mean(xf * xf, axis=-1, keepdims=True) + RMS_EPS)
    return (y * g.astype(jnp.float32)).astype(x.dtype)


def layernorm(x, g, b):
    xf = x.astype(jnp.float32)
    mu = jnp.mean(xf, axis=-1, keepdims=True)
    var = jnp.mean(jnp.square(xf - mu), axis=-1, keepdims=True)
    y = (xf - mu) * lax.rsqrt(var + LN_EPS)
    return (y * g.astype(jnp.float32) + b.astype(jnp.float32)).astype(x.dtype)


def neighborhood_attention(q, k, v, rpb):
    B, S, _ = q.shape
    rows = S // GRID_W
    kh = min(NA_ROWS, rows)
    kw = NA_COLS
    H, dh = NA_HEADS, NA_HEAD_DIM
    r = jnp.arange(rows)
    row_start = jnp.clip(r - kh // 2, 0, rows - kh)
    row_idx = row_start[:, None] + jnp.arange(kh)[None, :]
    c = jnp.arange(GRID_W)
    col_start = jnp.clip(c - kw // 2, 0, GRID_W - kw)
    col_ok = (c[None, :] >= col_start[:, None]) & (c[None, :] < col_start[:, None] + kw)
    qg = q.reshape(B, rows, GRID_W, H, dh)
    kband = jnp.take(k.reshape(B, rows, GRID_W, H, dh), row_idx, axis=1)
    vband = jnp.take(v.reshape(B, rows, GRID_W, H, dh), row_idx, axis=1)
    s = jnp.einsum('brqhd,brjkhd->bhrqjk', qg, kband).astype(jnp.float32) * (dh ** -0.5)
    dr = row_idx - r[:, None]
    dc = jnp.clip(c[None, :] - c[:, None], -(kw - 1), kw - 1)
    idx_r = (dr + NA_ROWS - 1)[:, None, :, None]
    idx_c = (dc + NA_COLS - 1)[None, :, None, :]
    bias = rpb[:, idx_r, idx_c].astype(jnp.float32)
    s = jnp.where(col_ok[None, None, None, :, None, :], s + bias[None], -jnp.inf)
    p = jax.nn.softmax(s.reshape(B, H, rows, GRID_W, kh * GRID_W), axis=-1)
    p = p.reshape(B, H, rows, GRID_W, kh, GRID_W).astype(v.dtype)
    out = jnp.einsum('bhrqjk,brjkhd->brqhd', p, vband)
    return out.reshape(B, S, H * dh)


def conformer_conv(a, b, dw_w, dw_b, ln_g, ln_b, pw):
    h = a * jax.nn.sigmoid(b)
    h = lax.conv_general_dilated(h, dw_w[:, None, :], window_strides=(1,),
                                 padding=[(CONV_WIDTH // 2, CONV_WIDTH // 2)],
                                 dimension_numbers=('NWC', 'WIO', 'NWC'),
                                 feature_group_count=h.shape[-1]) + dw_b
    h = jax.nn.silu(layernorm(h, ln_g, ln_b))
    return h @ pw


def multiscale_pool(xc, pool_w, pool_scale):
    B, S, _ = xc.shape
    xg = xc.reshape(B, S, POOL_GROUPS, POOL_GROUP_W).astype(jnp.float32)
    csum = jnp.concatenate([jnp.zeros((B, 1, POOL_GROUPS, POOL_GROUP_W), jnp.float32),
                            jnp.cumsum(xg, axis=1)], axis=1)
    t = jnp.arange(S)[:, None]
    w = jnp.array(POOL_WINDOWS, dtype=jnp.int32)[None, :]
    lo = jnp.clip(t - w // 2, 0, S)
    hi = jnp.clip(t - w // 2 + w, 0, S)
    g_idx = jnp.arange(POOL_GROUPS)[None, :]
    win_sum = csum[:, hi, g_idx] - csum[:, lo, g_idx]
    count = (hi - lo).astype(jnp.float32)[None, :, :, None]
    pooled = (win_sum / count - xg).astype(xc.dtype)
    y = jnp.einsum('bsgc,gcd->bsgd', pooled, pool_w).reshape(B, S, GROUP_W)
    return y * pool_scale


def spatial_gating(u, v, ln_g, ln_b, ws, bs):
    B, S, _ = u.shape
    v = layernorm(v, ln_g, ln_b)
    vc = v.reshape(B, S // SGU_CHUNK, SGU_CHUNK, SGU_HEADS, SGU_HEAD_DIM)
    mixed = jnp.einsum('hpq,bnqhd->bnphd', ws, vc) + bs.T[None, None, :, :, None]
    return u * mixed.reshape(B, S, GROUP_W)


def setup_inputs(seed: int = 0) -> dict:
    key = jax.random.key(seed)
    ks = jax.random.split(key, 20)
    f32 = jnp.float32
    nrm = lambda k, shape, s: jax.random.normal(k, shape, f32) * s
    return {
        "x": nrm(ks[0], (BATCH, SEQ, D_MODEL), 1.0),
        "norm_g": 1.0 + nrm(ks[1], (DEPTH, D_MODEL), 0.02),
        "w_in": nrm(ks[2], (DEPTH, D_MODEL, D_IN), D_MODEL ** -0.5),
        "na_rpb": nrm(ks[3], (DEPTH, NA_HEADS, 2 * NA_ROWS - 1, 2 * NA_COLS - 1), 0.1),
        "conv_dw_w": nrm(ks[4], (DEPTH, CONV_WIDTH, GROUP_W), CONV_WIDTH ** -0.5),
        "conv_dw_b": nrm(ks[5], (DEPTH, GROUP_W), 0.02),
        "conv_ln_g": 1.0 + nrm(ks[6], (DEPTH, GROUP_W), 0.02),
        "conv_ln_b": nrm(ks[7], (DEPTH, GROUP_W), 0.02),
        "conv_pw": nrm(ks[8], (DEPTH, GROUP_W, GROUP_W), GROUP_W ** -0.5),
        "pool_w": nrm(ks[9], (DEPTH, POOL_GROUPS, POOL_GROUP_W, POOL_GROUP_W), POOL_GROUP_W ** -0.5),
        "pool_scale": 1.0 + nrm(ks[10], (DEPTH, GROUP_W), 0.02),
        "sgu_ln_g": 1.0 + nrm(ks[11], (DEPTH, GROUP_W), 0.02),
        "sgu_ln_b": nrm(ks[12], (DEPTH, GROUP_W), 0.02),
        "sgu_w": nrm(ks[13], (DEPTH, SGU_HEADS, SGU_CHUNK, SGU_CHUNK), SGU_CHUNK ** -0.5),
        "sgu_b": 1.0 + nrm(ks[14], (DEPTH, SGU_HEADS, SGU_CHUNK), 0.02),
        "w_out": nrm(ks[15], (DEPTH, D_MIX, D_MODEL), D_MIX ** -0.5),
        "final_g": 1.0 + nrm(ks[16], (D_MODEL,), 0.02),
    }


def reference(x, norm_g, w_in, na_rpb, conv_dw_w, conv_dw_b, conv_ln_g, conv_ln_b, conv_pw,
              pool_w, pool_scale, sgu_ln_g, sgu_ln_b, sgu_w, sgu_b, w_out, final_g):
    split_at = [GROUP_W * i for i in range(1, 9)]
    for l in range(DEPTH):
        h = rmsnorm(x, norm_g[l])
        z = h @ w_in[l]
        qa, ka, va, glu_a, glu_b, pool_in, sgu_u, sgu_v, gates = jnp.split(z, split_at, axis=-1)
        y_a = neighborhood_attention(qa, ka, va, na_rpb[l])
        y_b = conformer_conv(glu_a, glu_b, conv_dw_w[l], conv_dw_b[l], conv_ln_g[l], conv_ln_b[l], conv_pw[l])
        y_c = multiscale_pool(pool_in, pool_w[l], pool_scale[l])
        y_d = spatial_gating(jax.nn.gelu(sgu_u), jax.nn.gelu(sgu_v), sgu_ln_g[l], sgu_ln_b[l], sgu_w[l], sgu_b[l])
        y = jnp.concatenate([y_a, y_b, y_c, y_d], axis=-1) * jax.nn.silu(gates)
        x = x + y @ w_out[l]
    return rmsnorm(x, final_g)
```

```python
import contextlib
import numpy as np
import ml_dtypes
import concourse.bass as bass
import concourse.mybir as mybir
from concourse.bass_utils import run_bass_kernel_spmd

F32 = mybir.dt.float32
BF16 = mybir.dt.bfloat16
AF = mybir.ActivationFunctionType
ALU = mybir.AluOpType
AX = mybir.AxisListType

S, D, DEPTH, NCORES = 2048, 1024, 2, 8
NT = 16
DIN = 3072
NVL = 76
NEG = -30000.0
NBLK = 11
TABW = 512 + 4 * 256
SPECIAL = {2: [(0, 5)], 3: [(0, 6), (1, 7)], 12: [(14, 8), (15, 9)], 13: [(15, 10)]}
BLOCKS = [(8, 6), (8, 7), (8, 8), (8, 9), (8, 10), (2, 0), (3, 0), (3, 1), (12, 14), (12, 15), (13, 15)]
POOL_W = (2, 4, 8, 16)


class Prog:
    def __init__(self, nc, plan=None):
        self.nc = nc
        self.plan = plan
        self.engs = {"pe": nc.tensor, "act": nc.scalar, "dve": nc.vector,
                     "pool": nc.gpsimd, "sp": nc.sync}
        self.n = 0
        self.info = []
        self.cnt = []
        self.needed = []
        self.lastw = {}
        self.readers = {}
        self.last_on = {}
        self.bar = None
        self.sems = {}
        self.counts = {}
        self.waited = {}

    def sem(self, key):
        if key not in self.sems:
            self.sems[key] = self.nc.alloc_semaphore("s_" + "_".join(str(k) for k in key))
        return self.sems[key]

    def add(self, eng, fn, r=(), w=(), dma=None, extra=(), force=()):
        idx = self.n
        self.n += 1
        hard = set(extra) | set(force)
        soft = set()
        if self.bar is not None:
            hard.add(self.bar)
        for t in r:
            lw = self.lastw.get(t)
            if lw is not None:
                hard.add(lw)
        for t in w:
            lw = self.lastw.get(t)
            if lw is not None:
                hard.add(lw)
            for rd in self.readers.get(t, ()):
                soft.add(rd)
        for t in w:
            self.lastw[t] = idx
            self.readers[t] = []
        for t in r:
            self.readers.setdefault(t, []).append(idx)
        keep = []
        for d in hard | soft:
            deng, ddma = self.info[d]
            if ddma is None and dma is None and deng == eng:
                if eng == "pe" and d not in force:
                    continue
            keep.append(d)
        self.info.append((eng, dma))
        self.needed.append(dma is not None)
        if dma is None:
            self.last_on[eng] = idx
        if self.plan is None:
            for d in keep:
                self.needed[d] = True
            self.cnt.append(None)
            return idx
        E = self.engs[eng]
        need = {}
        for d in keep:
            deng, ddma = self.info[d]
            key = ("d", ddma) if ddma is not None else ("e", deng)
            need[key] = max(need.get(key, 0), self.cnt[d])
        for key, val in need.items():
            wk = (eng, key)
            if self.waited.get(wk, 0) >= val:
                continue
            self.waited[wk] = val
            E.wait_ge(self.sem(key), val)
        ins = fn(E)
        if dma is not None:
            key = ("d", dma)
            self.counts[key] = self.counts.get(key, 0) + 16
            self.cnt.append(self.counts[key])
            ins.then_inc(self.sem(key), 16)
        elif self.plan.needed[idx]:
            key = ("e", eng)
            self.counts[key] = self.counts.get(key, 0) + 1
            self.cnt.append(self.counts[key])
            ins.then_inc(self.sem(key), 1)
        else:
            self.cnt.append(None)
        return idx

    def barrier(self, scratch):
        lasts = list(self.last_on.values())
        self.bar = None
        self.bar = self.add("dve", lambda e: e.memset(scratch, 0.0), extra=lasts, w=["BAR"])

    def finish(self, eng="sp"):
        E = self.engs[eng]
        for key, val in self.counts.items():
            if self.waited.get((eng, key), 0) < val:
                E.wait_ge(self.sem(key), val)


class Arena:
    def __init__(self, ap_f32, nwords):
        self.base, self.n, self.off = ap_f32, nwords, 0

    def alloc(self, cols, dt=F32):
        words = cols if dt == F32 else (cols + 1) // 2
        assert self.off + words <= self.n, ("arena overflow", self.off, words, self.n)
        v = self.base[:, self.off:self.off + words]
        self.off += words
        return v if dt == F32 else v.bitcast(BF16)

    def reset(self, off=0):
        self.off = off


def blocks(j):
    lo, hi = max(0, j - 2), min(15, j + 2)
    sp = SPECIAL.get(j, [])
    spm = {m for m, _ in sp}
    while lo in spm:
        lo += 1
    while hi in spm:
        hi -= 1
    return lo, hi, sp


def complete_after(m):
    return 3 if m == 0 else min(15, m + 2)


def build(depth=DEPTH, first_layer=0, final_norm=True, dbg=False):
    nc = bass.Bass("TRN2", target_bir_lowering=False)
    dr = lambda name, shape, dt=F32: nc.dram_tensor(name, shape, dt, kind="ExternalInput").ap()
    x_d = dr("x", [S, D])
    normg_d = dr("norm_g", [DEPTH, D])
    win_d = dr("w_in", [DEPTH, D, DIN])
    wout_d = dr("w_out", [DEPTH, D, D])
    fg_d = dr("final_g", [1, D])
    tab_d = dr("na_tab", [DEPTH, 8, 128, TABW])
    mask_d = dr("na_mask", [128, TABW])
    pvec_d = dr("pvec", [128, DEPTH * NVL + 34])
    pw_d = dr("conv_pw", [DEPTH, 256, 256])
    poolw_d = dr("pool_w", [DEPTH, 4, 64, 64])
    swt_d = dr("sgu_wT", [DEPTH, 128, 4, 128])
    sbb_d = dr("sgu_bb", [DEPTH, 128, 2, 128])
    ident_d = dr("ident", [128, 128], BF16)
    out_d = nc.dram_tensor("out", [S, D], F32, kind="ExternalOutput").ap()
    if dbg:
        dbg_y = nc.dram_tensor("dbg_y", [128, 8, S], BF16, kind="ExternalOutput").ap()
        dbg_h = nc.dram_tensor("dbg_h", [128, 8, S], BF16, kind="ExternalOutput").ap()

    st = contextlib.ExitStack()
    with st:
        sb = lambda name, shape, dt: st.enter_context(nc.sbuf_tensor(name, shape, dt))
        xres = sb("xres", [128, NT, D], F32)
        hT = sb("hT", [128, 8, S], BF16)
        yT = sb("yT", [128, 8, S], BF16)
        NSLOT = 6
        Wr = sb("Wr", [128, NSLOT, 8, 256], BF16)
        ident = sb("ident_sb", [128, 128], BF16)
        ones_bf = sb("ones_bf", [128, 128], BF16)
        ident32 = sb("ident32", [128, 128], F32)
        i2f = sb("i2f", [128, 64], F32)
        i2b = sb("i2b", [128, 64], BF16)
        onesm = sb("onesm", [128, 128], BF16)
        pvec = sb("pvec_sb", [128, DEPTH * NVL + 34], F32)
        maskb = sb("maskb", [128, TABW], BF16)
        cst = sb("cst", [128, 16], F32)
        ARW = 12424
        arena_t = sb("arena", [128, ARW], F32)
        ps = st.enter_context(nc.psum_tensor("ps", [128, 8, 512], F32))
        psflat = ps[:].rearrange("p b n -> p (b n)")
        AR = Arena(arena_t[:], ARW)

        scratch = cst[:, 2:3]
        PB = lambda b: ("ps", b)

        def body(P):
            _body(P)

        def _run():
            P1 = Prog(nc, plan=None)
            body(P1)
            P2 = Prog(nc, plan=P1)
            body(P2)
            assert P1.n == P2.n
            P2.finish()

        def _body(P):
            def norm_setup(g_row):
                d = {"ss": AR.alloc(16), "rstd": AR.alloc(16), "junk": AR.alloc(1024),
                     "hts": [AR.alloc(1024, BF16) for _ in range(8)],
                     "ob": [AR.alloc(1024) for _ in range(2)], "gbc": AR.alloc(D)}
                P.add("sp", lambda e: e.dma_start(out=d["gbc"], in_=g_row.partition_broadcast(128)), w=["gbc"], dma="g")
                return d

            def norm_stats_act(d, t):
                ss, rstd = d["ss"], d["rstd"]
                P.add("act", lambda e: e.activation(out=d["junk"], in_=xres[:, t, :], func=AF.Square, accum_out=ss[:, t:t + 1]),
                      r=[("x", t)], w=[("ss", t), "junk"])
                P.add("act", lambda e: e.activation(out=rstd[:, t:t + 1], in_=ss[:, t:t + 1], func=AF.Sqrt, bias=cst[:, 0:1], scale=1.0 / D),
                      r=[("ss", t), "cst"], w=[("rstd", t)])

            def norm_stats_dve(d, t):
                rstd = d["rstd"]
                P.add("dve", lambda e: e.reciprocal(out=rstd[:, t:t + 1], in_=rstd[:, t:t + 1]), r=[("rstd", t)], w=[("rstd", t)])

            def norm_stats(d, t):
                norm_stats_act(d, t)
                norm_stats_dve(d, t)

            def normA_h(d, t):
                hb = d["hts"][t % 8]
                P.add("dve", lambda e: e.scalar_tensor_tensor(out=hb, in0=xres[:, t, :], scalar=d["rstd"][:, t:t + 1], in1=d["gbc"],
                                                              op0=ALU.mult, op1=ALU.mult),
                      r=[("x", t), ("rstd", t), "gbc"], w=[("ht", t % 8)])

            def normA_tile(d, t, after_stt=None):
                normA_h(d, t)
                if after_stt is not None:
                    after_stt()
                normA_T(d, t)

            def normA_T(d, t):
                hb = d["hts"][t % 8]
                bk = t % 4
                psb = ps[:, bk, :].bitcast(BF16)
                for c in range(8):
                    P.add("pe", lambda e, c=c: e.transpose(out=psb[:, c * 128:(c + 1) * 128], in_=hb[:, c * 128:(c + 1) * 128], identity=ident[:]),
                          r=[("ht", t % 8), "ident"], w=[PB(bk)])
                P.add("act", lambda e: e.activation(out=hT[:, :, t * 128:(t + 1) * 128],
                                                    in_=psb.rearrange("p (c n) -> p c n", c=8), func=AF.Copy),
                      r=[PB(bk)], w=["hT"])

            def final_tile(d, t):
                o = d["ob"][t % 2]
                P.add("dve", lambda e: e.scalar_tensor_tensor(out=o, in0=xres[:, t, :], scalar=d["rstd"][:, t:t + 1], in1=d["gbc"], op0=ALU.mult, op1=ALU.mult),
                      r=[("x", t), ("rstd", t), "gbc"], w=[("ob", t % 2)])
                P.add("sp", lambda e: e.dma_start(out=out_d[t * 128:(t + 1) * 128, :], in_=o), r=[("ob", t % 2)], dma=f"o{t % 2}")

            P.add("sp", lambda e: e.dma_start(out=ident[:], in_=ident_d), w=["ident"], dma="c0")
            P.add("sp", lambda e: e.dma_start(out=pvec[:], in_=pvec_d), w=["pvec"], dma="c1")
            AR.reset()
            nrm_first = norm_setup(normg_d[first_layer:first_layer + 1, :])
            xload = []
            for t in range(NT):
                xload.append(P.add("sp" if t % 2 == 0 else "act", lambda e, t=t: e.dma_start(out=xres[:, t, :], in_=x_d[t * 128:(t + 1) * 128, :]),
                                   w=[("x", t)], dma=f"x{t}"))
            P.add("pool", lambda e: e.dma_start(out=maskb[:], in_=mask_d), w=["maskb"], dma="c2")
            P.add("dve", lambda e: e.memset(ones_bf[:], 1.0), w=["ones"])
            P.add("dve", lambda e: e.tensor_copy(out=ident32[:], in_=ident[:]), r=["ident"], w=["ident32"])
            P.add("dve", lambda e: e.tensor_tensor(out=i2f[:], in0=ident32[:, 0:64], in1=ident32[:, 64:128], op=ALU.add), r=["ident32"], w=["i2"])
            P.add("dve", lambda e: e.tensor_copy(out=i2b[:], in_=i2f[:]), r=["i2"], w=["i2"])
            P.add("dve", lambda e: e.memset(onesm[:], 1.0 / 256.0), w=["onesm"])
            P.add("dve", lambda e: e.memset(cst[:, 0:1], 1e-6), w=["cst"])
            P.add("dve", lambda e: e.memset(cst[:, 1:2], 1e-5), w=["cst"])

            units = []
            for l in range(first_layer, first_layer + depth):
                for c0 in (0, 256, 512, 2048,
                           768, 1024, 2304,
                           1280, 2560,
                           1536, 1792, 2816):
                    units.append(("in", l, c0))
                for c0 in (0, 256, 512, 768):
                    units.append(("out", l, c0))
            UPL = 16
            issued = [0]

            def load_unit(i):
                kind, l, c0 = units[i]
                slot = i % NSLOT
                if kind == "in":
                    src = win_d[l, :, c0:c0 + 256].rearrange("(c p) n -> p c n", p=128)
                    P.add("pool", lambda e: e.dma_start(out=Wr[:, slot], in_=src), w=[("W", slot)], dma=f"W{slot}",
                          extra=([xload[7]] if i < 4 else []))
                elif c0 % 512 == 256:
                    s0 = slot - 1
                    assert s0 % 2 == 0 and s0 >= 0
                    src = wout_d[l, :, c0 - 256:c0 + 256].rearrange("(c p) n -> p c n", p=128)
                    P.add("pool", lambda e: e.dma_start(out=wo_view(s0), in_=src), w=[("W", s0), ("W", s0 + 1)], dma=f"W{s0}")

            def wo_view(slot):
                return Wr[:, slot:slot + 2].rearrange("p s c n -> p (s c n)").rearrange("p (c m) -> p c m", c=8)

            def done_unit(i):
                nxt = i + NSLOT
                while issued[0] <= nxt and issued[0] < len(units):
                    load_unit(issued[0])
                    issued[0] += 1

            for i in range(4):
                load_unit(i)
            issued[0] = 4

            bank_rr = [0]

            def nextbank():
                b = bank_rr[0]
                bank_rr[0] = (b + 1) % 8
                return b

            def proj_fm(ui, col, n, bank):
                slot = ui % NSLOT
                for c in range(8):
                    P.add("pe", lambda e, c=c: e.matmul(ps[:, bank, :], lhsT=Wr[:, slot, c, col:col + 128],
                                                        rhs=hT[:, c, n * 512:(n + 1) * 512], start=(c == 0), stop=(c == 7)),
                          r=[("W", slot), "hT", "BAR"], w=[PB(bank)])

            def proj_tm(ui, col, ncol, t, bank, off):
                slot = ui % NSLOT
                for c in range(8):
                    P.add("pe", lambda e, c=c: e.matmul(ps[:, bank, off:off + ncol], lhsT=hT[:, c, t * 128:(t + 1) * 128],
                                                        rhs=Wr[:, slot, c, col:col + ncol], start=(c == 0), stop=(c == 7)),
                          r=[("W", slot), "hT", "BAR"], w=[PB(bank)])

            for li in range(depth):
                l = first_layer + li
                ub = li * UPL
                pv = lambda k: pvec[:, l * NVL + k:l * NVL + k + 1]

                if li == 0:
                    nrm = nrm_first
                    norm_stats(nrm, 0)
                    norm_stats(nrm, 1)
                    for t in range(NT):
                        if t + 2 < NT:
                            norm_stats_act(nrm, t + 2)
                            normA_tile(nrm, t, after_stt=lambda t=t: norm_stats_dve(nrm, t + 2))
                        else:
                            normA_tile(nrm, t)
                if dbg and li == 0:
                    P.add("sp", lambda e: e.dma_start(out=dbg_h, in_=hT[:]), r=["hT"], dma="dbg")
                if li == 0:
                    while issued[0] < min(NSLOT, len(units)):
                        load_unit(issued[0])
                        issued[0] += 1
                P.barrier(scratch)

                uq, uk, uv, ug = ub + 0, ub + 1, ub + 2, ub + 3
                for g in range(2):
                    AR.reset()
                    qT = AR.alloc(S, BF16)
                    kTm = [AR.alloc(S, BF16) for _ in range(2)]
                    gT = AR.alloc(S, BF16)
                    vtok = AR.alloc(NT * 128, BF16).rearrange("p (t n) -> p t n", t=NT)
                    E = AR.alloc(4 * TABW, BF16).rearrange("p (h n) -> p h n", h=4)
                    NPT = 8
                    PT = [AR.alloc(512, BF16) for _ in range(NPT)]
                    tyrc = [AR.alloc(256).rearrange("p (a n) -> p a n", a=2) for _ in range(4)]
                    tyb = [t_[:, 0, :] for t_ in tyrc]
                    rcb = [t_[:, 1, :] for t_ in tyrc]
                    P.add("pool", lambda e, g=g: e.dma_start(out=E, in_=tab_d[l, 4 * g:4 * g + 4].rearrange("h p n -> p h n")),
                          r=["BAR"], w=["E"] + [("E", hh_) for hh_ in range(4)], dma="E")
                    def e_prep(hh):
                        P.add("dve", lambda e: e.tensor_tensor(out=E[:, hh, :], in0=E[:, hh, :], in1=maskb[:], op=ALU.add),
                              r=["E", "maskb"], w=[("E", hh)])
                        P.add("act", lambda e: e.activation(out=E[:, hh, :], in_=E[:, hh, :], func=AF.Exp),
                              r=[("E", hh)], w=[("E", hh)])

                    for n in range(4):
                        if n >= 2:
                            e_prep(2 * (n - 2))
                            e_prep(2 * (n - 2) + 1)
                        b = nextbank()
                        proj_fm(uq, g * 128, n, b)
                        P.add("dve", lambda e, n=n, b=b: e.tensor_scalar(out=qT[:, n * 512:(n + 1) * 512], in0=ps[:, b, :], scalar1=32.0 ** -0.5, scalar2=None, op0=ALU.mult),
                              r=[PB(b)], w=["qT"])
                        b = nextbank()
                        proj_fm(uk, g * 128, n, b)
                        for par in range(2):
                            mcol = pvec[:, DEPTH * NVL + 32 + par:DEPTH * NVL + 33 + par]
                            P.add("act", lambda e, n=n, b=b, par=par, mcol=mcol: e.activation(out=kTm[par][:, n * 512:(n + 1) * 512], in_=ps[:, b, :], func=AF.Identity, scale=mcol),
                                  r=[PB(b), "pvec"], w=["kT"])
                        b = nextbank()
                        proj_fm(ug, g * 128, n, b)
                        P.add("act", lambda e, n=n, b=b: e.activation(out=gT[:, n * 512:(n + 1) * 512], in_=ps[:, b, :], func=AF.Silu),
                              r=[PB(b)], w=["gT"])
                    for t4 in range(4):
                        b = nextbank()
                        for tt in range(4):
                            proj_tm(uv, g * 128, 128, 4 * t4 + tt, b, tt * 128)
                        P.add("act", lambda e, t4=t4, b=b: e.activation(out=vtok[:, 4 * t4:4 * t4 + 4, :], in_=ps[:, b, :].rearrange("p (t n) -> p t n", t=4), func=AF.Copy),
                              r=[PB(b)], w=["vtok"])
                    if g == 1:
                        for u in (uq, uk, uv, ug):
                            done_unit(u)
                    for b in range(4, 8):
                        P.add("dve", lambda e, b=b: e.memset(ps[:, b, :], 0.0), w=[PB(b)])
                    accy = psflat[:, 4 * 512:6 * 512]
                    accs = psflat[:, 6 * 512:8 * 512]
                    acc2 = psflat[:, 4 * 512:8 * 512].rearrange("p (a n) -> p a n", a=2)
                    ACCY = lambda m: PB(4 + (m % 8) // 4)
                    ACCS = lambda m: PB(6 + (m % 8) // 4)
                    MULENG = {0: "dve", 1: "dve", 2: "dve", 3: "dve"}
                    seq = []
                    for j in range(16):
                        seq.append(("reg", j))
                        if j == 3:
                            seq.append(("xt", 0))
                        if j == 13:
                            seq.append(("xb", 0))
                    pos_of = {kt: i for i, kt in enumerate(seq)}

                    def parts_of(kind, j):
                        out = []
                        if kind == "reg":
                            for h in range(2):
                                kr = 2 * j + h
                                lo_, hi_ = max(0, kr - 3), min(31, kr + 4)
                                out.append((j, h, lo_, hi_, (lo_ - (kr - 3)) * 64))
                            return out, 0
                        jks, rows, t0 = ((2, 3), (0, 3), 512) if kind == "xt" else ((12, 13), (28, 31), 1024)
                        for k, jk in enumerate(jks):
                            for h in range(2):
                                out.append((jk, h, rows[0], rows[1], 256 * k))
                        return out, t0

                    last_pos = {}
                    for m in range(16):
                        cand = [pos_of[("reg", j)] for j in range(16) if max(0, 2 * j - 3) <= 2 * m + 1 and min(31, 2 * j + 5) >= 2 * m]
                        if m <= 1:
                            cand.append(pos_of[("xt", 0)])
                        if m >= 14:
                            cand.append(pos_of[("xb", 0)])
                        last_pos[m] = max(cand)

                    def s_burst(i):
                        kind, j = seq[i]
                        pts, t0 = parts_of(kind, j)
                        c_lo = min(c0 for (_, _, _, _, c0) in pts)
                        c_hi = max(c0 + (hi_ - lo_ + 1) * 64 for (_, _, lo_, hi_, c0) in pts)
                        for par in range(2):
                            for (jk, h, lo_, hi_, c0) in pts:
                                n_ = (hi_ - lo_ + 1) * 64
                                for sb_ in range(2):
                                    for g2 in range(2):
                                        hh = 2 * g2 + par
                                        rs_ = slice(64 * g2, 64 * g2 + 64)
                                        k0 = jk * 128 + 64 * h + 32 * sb_
                                        o0 = 64 * h + 32 * sb_
                                        P.add("pe", lambda e, hh=hh, rs_=rs_, k0=k0, o0=o0, lo_=lo_, c0=c0, n_=n_, par=par, g2=g2: e.matmul(
                                            ps[o0:o0 + 32, hh, c0:c0 + n_], lhsT=kTm[par][rs_, k0:k0 + 32], rhs=qT[rs_, lo_ * 64:lo_ * 64 + n_],
                                            start=True, stop=True, skip_group_check=True, tile_position=(64 * g2, o0)),
                                            r=["qT", "kT"], w=[PB(hh)])
                        for hh in range(4):
                            pt = PT[(4 * i + hh) % NPT]
                            ptk = ("PT", (4 * i + hh) % NPT)
                            P.add("act", lambda e, hh=hh, pt=pt: e.activation(out=pt[:, c_lo:c_hi], in_=ps[:, hh, c_lo:c_hi], func=AF.Exp),
                                  r=[PB(hh)], w=[ptk])
                            P.add(MULENG[hh], lambda e, hh=hh, pt=pt: e.tensor_tensor(out=pt[:, c_lo:c_hi], in0=pt[:, c_lo:c_hi], in1=E[:, hh, t0 + c_lo:t0 + c_hi], op=ALU.mult),
                                  r=[ptk, ("E", hh)], w=[ptk])

                    pvlast = [None]

                    def pv_block(i):
                        kind, j = seq[i]
                        pts, _t0 = parts_of(kind, j)
                        items = {0: {0: [], 1: []}, 1: {0: [], 1: []}}
                        for (jk, h, lo_, hi_, c0) in pts:
                            r0 = lo_
                            while r0 <= hi_:
                                r1 = min(hi_, (r0 // 8) * 8 + 7)
                                for which in range(2):
                                    items[h][which].append((jk, lo_, c0, r0, r1))
                                r0 = r1 + 1

                        def group(h, which, jk, lo_, c0, r0, r1, frc):
                            wdt = (r1 - r0 + 1) * 64
                            oc = (r0 % 16) * 64
                            pc = c0 + (r0 - lo_) * 64
                            bank = (4 if which == 0 else 6) + (r0 % 16) // 8
                            for hh in range(4):
                                pt = PT[(4 * i + hh) % NPT]
                                hs = slice(32 * hh, 32 * hh + 32)
                                acc = accy if which == 0 else accs
                                lhs = vtok[64 * h:64 * h + 64, jk, hs] if which == 0 else ones_bf[64 * h:64 * h + 64, 0:32]
                                pvlast[0] = P.add("pe", lambda e, acc=acc, lhs=lhs, pt=pt, hs=hs, hh=hh: e.matmul(
                                    acc[hs, oc:oc + wdt], lhsT=lhs, rhs=pt[64 * h:64 * h + 64, pc:pc + wdt],
                                    start=False, stop=True, skip_group_check=True, tile_position=(64 * h, 32 * hh)),
                                    r=[("PT", (4 * i + hh) % NPT), "vtok", "ones"], w=[PB(bank)], force=(frc if hh == 0 else []))
                                frc = []

                        for phase in range(2):
                            l0 = items[0][phase]
                            l1 = items[1][1 - phase]
                            frc = [pvlast[0]] if (phase == 1 and pvlast[0] is not None) else []
                            for k in range(max(len(l0), len(l1))):
                                if k < len(l0):
                                    group(0, phase, *l0[k], frc)
                                    frc = []
                                if k < len(l1):
                                    group(1, 1 - phase, *l1[k], frc)
                                    frc = []
                        for m in range(16):
                            if last_pos[m] != i:
                                continue
                            oc = (m % 8) * 128
                            rc, ty = rcb[m % 4], tyb[m % 4]
                            P.add("dve", lambda e, m=m, oc=oc: e.tensor_copy(out=tyrc[m % 4], in_=acc2[:, :, oc:oc + 128]),
                                  r=[ACCY(m), ACCS(m)], w=[("ty", m % 4), ("rc", m % 4)])
                            P.add("dve", lambda e, oc=oc: e.memset(acc2[:, :, oc:oc + 128], 0.0),
                                  r=[ACCY(m), ACCS(m)], w=[ACCY(m), ACCS(m)])
                            def post(rc=rc, ty=ty, m=m):
                                P.add("act", lambda e: e.activation(out=rc, in_=rc, func=AF.Ln), r=[("rc", m % 4)], w=[("rc", m % 4)])
                                P.add("act", lambda e: e.activation(out=rc, in_=rc, func=AF.Exp, scale=-1.0), r=[("rc", m % 4)], w=[("rc", m % 4)])
                                P.add("dve", lambda e: e.tensor_tensor(out=ty, in0=ty, in1=rc, op=ALU.mult),
                                      r=[("ty", m % 4), ("rc", m % 4)], w=[("ty", m % 4)])
                                P.add("dve", lambda e: e.tensor_tensor(out=yT[:, g, m * 128:(m + 1) * 128], in0=ty, in1=gT[:, m * 128:(m + 1) * 128], op=ALU.mult),
                                      r=[("ty", m % 4), "gT"], w=["yT"])
                            posts.append(post)

                    posts = []
                    for i in range(len(seq)):
                        s_burst(i)
                        for p_ in posts:
                            p_()
                        del posts[:]
                        if i > 0:
                            pv_block(i - 1)
                    pv_block(len(seq) - 1)
                    for p_ in posts:
                        p_()
                    del posts[:]
                    if g == 1:
                        P.barrier(scratch)

                ua, ubb, ugb = ub + 4, ub + 5, ub + 6
                AR.reset()
                GLW = S + 30
                gl = AR.alloc(2 * GLW, BF16).rearrange("p (c n) -> p c n", c=2)
                gl0 = AR.alloc(2 * GLW, BF16).rearrange("p (c n) -> p c n", c=2)
                dg = AR.alloc(64 * 64, BF16).rearrange("p (k n) -> p k n", k=64)
                PWs = AR.alloc(2 * 256, BF16).rearrange("p (c n) -> p c n", c=2)
                sig = [AR.alloc(512) for _ in range(2)]
                c32s = [AR.alloc(2 * 512).rearrange("p (c n) -> p c n", c=2) for _ in range(2)]
                cbf = AR.alloc(2 * 512, BF16).rearrange("p (c n) -> p c n", c=2)
                sqb = AR.alloc(2 * 512, BF16).rearrange("p (c n) -> p c n", c=2)
                mean = AR.alloc(512)
                msq = AR.alloc(512)
                rs = msq
                hsb = AR.alloc(2 * 512, BF16).rearrange("p (c n) -> p c n", c=2)
                P.add("pool", lambda e: e.dma_start(out=PWs, in_=pw_d[l].rearrange("(c p) n -> p c n", p=128)), r=["BAR"], w=["PWs"], dma="pw")
                P.add("pool", lambda e: e.memset(gl[:, :, 0:15], 0.0), r=["BAR"], w=["glpad"])
                P.add("pool", lambda e: e.memset(gl[:, :, 15 + S:GLW], 0.0), r=["BAR"], w=["glpad"])
                P.add("pool", lambda e: e.memset(gl0[:, :, GLW - 1:GLW], 0.0), r=["BAR"], w=["gl0pad", "gl0"])
                dgi = [0]

                def diag_some(k):
                    for _ in range(k):
                        i = dgi[0]
                        if i >= 64:
                            return
                        dgi[0] += 1
                        if i % 2 == 0:
                            P.add("act", lambda e, i=i: e.activation(out=dg[:, i, :], in_=i2b[:], func=AF.Identity, scale=pv(12 + i)),
                                  r=["i2", "pvec"], w=["dg"])
                        else:
                            P.add("dve", lambda e, i=i: e.tensor_scalar(out=dg[:, i, :], in0=i2f[:], scalar1=pv(12 + i), scalar2=None, op0=ALU.mult),
                                  r=["i2", "pvec"], w=["dg"])

                for cc in range(2):
                    for n in range(4):
                        b = nextbank()
                        proj_fm(ubb, cc * 128, n, b)
                        sg = sig[n % 2]
                        P.add("act", lambda e, b=b, sg=sg: e.activation(out=sg, in_=ps[:, b, :], func=AF.Sigmoid),
                              r=[PB(b)], w=[("sig", n % 2)])
                        b = nextbank()
                        proj_fm(ua, cc * 128, n, b)
                        P.add("dve", lambda e, b=b, sg=sg, cc=cc, n=n: e.tensor_tensor(out=gl[:, cc, 15 + n * 512:15 + (n + 1) * 512], in0=ps[:, b, :], in1=sg, op=ALU.mult),
                              r=[PB(b), ("sig", n % 2)], w=["gl"])
                        diag_some(8)
                    P.add("dve", lambda e, cc=cc: e.tensor_copy(out=gl0[0:64, cc, :], in_=gl[0:64, cc, :]), r=["gl", "glpad"], w=["gl0"])
                    P.add("dve", lambda e, cc=cc: e.tensor_copy(out=gl0[64:128, cc, 0:GLW - 1], in_=gl[0:64, cc, 1:GLW]), r=["gl", "glpad"], w=["gl0"])
                    P.add("dve", lambda e, cc=cc: e.tensor_copy(out=gl[0:64, cc, 0:GLW - 1], in_=gl[64:128, cc, 1:GLW]), r=["gl", "glpad", "gl0"], w=["gl"])
                for u in (ua, ubb):
                    done_unit(u)

                def conv_stage(n):
                    c32 = c32s[n % 2]
                    cb = [nextbank(), nextbank()]
                    for cc in range(2):
                        for j in range(16):
                            for a in range(2):
                                src = gl0 if a == 0 else gl
                                P.add("pe", lambda e, cc=cc, j=j, a=a, src=src, b=cb[cc]: e.matmul(
                                    ps[64 * a:64 * a + 64, b, :], lhsT=dg[:, (cc * 2 + a) * 16 + j, :], rhs=src[:, cc, n * 512 + 2 * j:n * 512 + 2 * j + 512],
                                    start=(j == 0), stop=(j == 15), skip_group_check=True, tile_position=(0, 64 * a)),
                                    r=["dg", "gl", "gl0", "glpad", "gl0pad"], w=[PB(cb[cc])])
                        P.add("dve", lambda e, cc=cc, b=cb[cc]: e.tensor_scalar(out=c32[:, cc, :], in0=ps[:, b, :], scalar1=pv(0 + cc), scalar2=None, op0=ALU.add),
                              r=[PB(cb[cc]), "pvec"], w=[("c32", n % 2, cc)])

                sbanks = {}

                def stats_a1(n):
                    c32 = c32s[n % 2]
                    for cc in range(2):
                        P.add("act", lambda e, cc=cc: e.activation(out=cbf[:, cc, :], in_=c32[:, cc, :], func=AF.Copy), r=[("c32", n % 2, cc)], w=[("cbf", cc)])
                        P.add("act", lambda e, cc=cc: e.activation(out=sqb[:, cc, :], in_=c32[:, cc, :], func=AF.Square), r=[("c32", n % 2, cc)], w=[("sqb", cc)])

                def stats_a2(n):
                    bm, be = nextbank(), nextbank()
                    sbanks[n] = (bm, be)
                    for cc in range(2):
                        P.add("pe", lambda e, cc=cc: e.matmul(ps[:, bm, :], lhsT=onesm[:], rhs=cbf[:, cc, :], start=(cc == 0), stop=(cc == 1)),
                              r=["onesm", ("cbf", cc)], w=[PB(bm)])
                    for cc in range(2):
                        P.add("pe", lambda e, cc=cc: e.matmul(ps[:, be, :], lhsT=onesm[:], rhs=sqb[:, cc, :], start=(cc == 0), stop=(cc == 1)),
                              r=["onesm", ("sqb", cc)], w=[PB(be)])

                def stats_b(n):
                    c32 = c32s[n % 2]
                    bm, be = sbanks[n]
                    P.add("act", lambda e: e.activation(out=msq, in_=ps[:, bm, :], func=AF.Square), r=[PB(bm)], w=["msq"])
                    P.add("dve", lambda e: e.tensor_tensor(out=rs, in0=ps[:, be, :], in1=msq, op=ALU.subtract), r=[PB(be), "msq"], w=["msq"])
                    P.add("act", lambda e: e.activation(out=rs, in_=rs, func=AF.Ln, bias=cst[:, 1:2], scale=1.0), r=["msq", "cst"], w=["msq"])
                    P.add("act", lambda e: e.activation(out=rs, in_=rs, func=AF.Exp, scale=-0.5), r=["msq"], w=["msq"])
                    for cc in range(2):
                        P.add("dve", lambda e, cc=cc: e.tensor_tensor(out=c32[:, cc, :], in0=c32[:, cc, :], in1=ps[:, bm, :], op=ALU.subtract), r=[("c32", n % 2, cc), PB(bm), "msq"], w=[("c32", n % 2, cc)])
                        P.add("dve", lambda e, cc=cc: e.tensor_tensor(out=c32[:, cc, :], in0=c32[:, cc, :], in1=rs, op=ALU.mult), r=[("c32", n % 2, cc), "msq"], w=[("c32", n % 2, cc)])
                        P.add("act", lambda e, cc=cc: e.activation(out=hsb[:, cc, :], in_=c32[:, cc, :], func=AF.Silu, scale=pv(2 + cc), bias=pv(4 + cc)),
                              r=[("c32", n % 2, cc), "pvec"], w=[("hsb", cc)])

                def pw_stage(n):
                    for fo in range(2):
                        b2 = nextbank()
                        proj_fm(ugb, fo * 128, n, b2)
                        sg = sig[fo]
                        P.add("act", lambda e, b2=b2, sg=sg: e.activation(out=sg, in_=ps[:, b2, :], func=AF.Silu),
                              r=[PB(b2)], w=[("sig", fo)])
                        b = nextbank()
                        for cc in range(2):
                            P.add("pe", lambda e, cc=cc, fo=fo, b=b: e.matmul(ps[:, b, :], lhsT=PWs[:, cc, fo * 128:(fo + 1) * 128], rhs=hsb[:, cc, :], start=(cc == 0), stop=(cc == 1)),
                                  r=["PWs", ("hsb", cc)], w=[PB(b)])
                        P.add("dve", lambda e, fo=fo, b=b, sg=sg: e.tensor_tensor(out=yT[:, 2 + fo, n * 512:(n + 1) * 512], in0=ps[:, b, :], in1=sg, op=ALU.mult),
                              r=[PB(b), ("sig", fo)], w=["yT"])

                conv_stage(0)
                stats_a1(0)
                stats_a2(0)
                stats_b(0)
                for n in range(4):
                    if n + 1 < 4:
                        conv_stage(n + 1)
                        stats_a1(n + 1)
                    pw_stage(n)
                    if n + 1 < 4:
                        stats_a2(n + 1)
                        stats_b(n + 1)
                done_unit(ugb)
                P.barrier(scratch)

                up, ugc = ub + 7, ub + 8
                AR.reset()
                XW = S + 16
                gTc = AR.alloc(2 * S, BF16).rearrange("p (c n) -> p c n", c=2)
                PWt = AR.alloc(2 * 128, BF16).rearrange("p (c n) -> p c n", c=2)
                Xs = [AR.alloc(XW) for _ in range(2)]
                A_ = AR.alloc(XW)
                B_ = AR.alloc(XW)
                plb1 = AR.alloc(S, BF16)
                plbs = [plb1, Xs[0][:, 0:S // 2].bitcast(BF16)]
                P.add("pool", lambda e: e.memset(PWt, 0.0), w=["PWt"])
                for cc in range(2):
                    for hf in range(2):
                        P.add("pool", lambda e, cc=cc, hf=hf: e.dma_start(out=PWt[64 * hf:64 * hf + 64, cc, 64 * hf:64 * hf + 64], in_=poolw_d[l, 2 * cc + hf]),
                              r=["PWt"], w=["PWt"], dma="pwt")
                for cc in range(2):
                    P.add("pool", lambda e, cc=cc: e.memset(Xs[cc][:, 0:8], 0.0), w=[("Xpad", cc)])
                    P.add("pool", lambda e, cc=cc: e.memset(Xs[cc][:, 8 + S:XW], 0.0), w=[("Xpad", cc)])
                for cc in range(2):
                    for n in range(4):
                        b = nextbank()
                        proj_fm(up, cc * 128, n, b)
                        P.add("act", lambda e, b=b, n=n, cc=cc: e.activation(out=Xs[cc][:, 8 + n * 512:8 + (n + 1) * 512], in_=ps[:, b, :], func=AF.Copy),
                              r=[PB(b)], w=[("X", cc)])
                for cc in range(2):
                    for n in range(4):
                        b = nextbank()
                        proj_fm(ugc, cc * 128, n, b)
                        P.add("act", lambda e, b=b, cc=cc, n=n: e.activation(out=gTc[:, cc, n * 512:(n + 1) * 512], in_=ps[:, b, :], func=AF.Silu),
                              r=[PB(b)], w=["gTc"])
                done_unit(up)
                done_unit(ugc)
                def c_sums(cc):
                    X = Xs[cc]
                    add = lambda o, a, b_, tok_r, tok_w: P.add("dve", lambda e: e.tensor_tensor(out=o, in0=a, in1=b_, op=ALU.add), r=tok_r, w=tok_w)
                    xt = [("X", cc), ("Xpad", cc)]
                    add(A_[:, 1:XW], X[:, 0:XW - 1], X[:, 1:XW], xt, ["A"])
                    if cc == 0:
                        add(B_[64:128, 2:XW - 1], A_[64:128, 1:XW - 2], A_[64:128, 3:XW], ["A"], ["B"])
                    else:
                        add(B_[:, 2:XW - 1], A_[:, 1:XW - 2], A_[:, 3:XW], ["A"], ["B"])
                        add(A_[:, 4:XW - 3], B_[:, 2:XW - 5], B_[:, 6:XW - 1], ["B"], ["A"])
                        add(B_[64:128, 8:XW - 8], A_[64:128, 4:XW - 12], A_[64:128, 12:XW - 4], ["A"], ["B"])

                def c_pool(cc):
                    X = Xs[cc]
                    plb = plbs[cc]
                    srcs = [(A_, 2.0), (B_, 4.0)] if cc == 0 else [(A_, 8.0), (B_, 16.0)]
                    for hf in range(2):
                        src, wv = srcs[hf]
                        pr = slice(64 * hf, 64 * hf + 64)
                        for (e0, tcol) in ((8, 0), (8 + S - 8, 8)):
                            dc = DEPTH * NVL + cc * 16 + tcol
                            P.add("dve", lambda e, src=src, pr=pr, e0=e0, dc=dc: e.tensor_tensor(out=src[pr, e0:e0 + 8], in0=src[pr, e0:e0 + 8], in1=pvec[pr, dc:dc + 8], op=ALU.mult),
                                  r=["A", "B", "pvec"], w=["A", "B"])
                        P.add("dve", lambda e, src=src, pr=pr, wv=wv: e.scalar_tensor_tensor(out=plb[pr, :], in0=src[pr, 8:8 + S], scalar=1.0 / wv, in1=X[pr, 8:8 + S],
                                                                                             op0=ALU.mult, op1=ALU.subtract),
                              r=["A", "B", ("X", cc)], w=[("plb", cc)] + ([("X", 0)] if cc == 1 else []))

                def c_out(cc):
                    plb = plbs[cc]
                    for n in range(4):
                        b = nextbank()
                        P.add("pe", lambda e, n=n, b=b: e.matmul(ps[:, b, :], lhsT=PWt[:, cc, :], rhs=plb[:, n * 512:(n + 1) * 512], start=True, stop=True),
                              r=["PWt", ("plb", cc)], w=[PB(b)])
                        P.add("dve", lambda e, n=n, b=b: e.scalar_tensor_tensor(out=yT[:, 4 + cc, n * 512:(n + 1) * 512], in0=ps[:, b, :], scalar=pv(6 + cc),
                                                                              in1=gTc[:, cc, n * 512:(n + 1) * 512], op0=ALU.mult, op1=ALU.mult),
                              r=[PB(b), "pvec", "gTc"], w=["yT"])

                c_sums(0)
                c_pool(0)
                c_sums(1)
                c_pool(1)
                c_out(0)
                c_out(1)
                P.barrier(scratch)

                uu, uvv, ugd = ub + 9, ub + 10, ub + 11
                AR.reset()
                guT = AR.alloc(2 * S, BF16).rearrange("p (c n) -> p c n", c=2)
                gv = AR.alloc(NT * 256).rearrange("p (t n) -> p t n", t=NT)
                vn = AR.alloc(NT * 256, BF16).rearrange("p (t n) -> p t n", t=NT)
                wsT = AR.alloc(4 * 128, BF16).rearrange("p (h n) -> p h n", h=4)
                bsb = AR.alloc(2 * 128).rearrange("p (c n) -> p c n", c=2)
                bias2 = AR.alloc(2 * 512).rearrange("p (c q n) -> p c q n", c=2, q=4)
                gsm = [AR.alloc(512, BF16) for _ in range(2)]
                s1 = AR.alloc(16)
                s2 = AR.alloc(16)
                mu = AR.alloc(16)
                rsd = AR.alloc(16)
                jk = AR.alloc(256)
                tmx = [AR.alloc(512) for _ in range(2)]
                P.add("pool", lambda e: e.dma_start(out=wsT, in_=swt_d[l]), r=["BAR"], w=["wsT"], dma="wsT")
                P.add("sp", lambda e: e.dma_start(out=bsb, in_=sbb_d[l]), r=["BAR"], w=["bsb"], dma="bsb")
                for cc in range(2):
                    for n in range(4):
                        b = nextbank()
                        proj_fm(uu, cc * 128, n, b)
                        P.add("act", lambda e, b=b, cc=cc, n=n: e.activation(out=guT[:, cc, n * 512:(n + 1) * 512], in_=ps[:, b, :], func=AF.Gelu_apprx_tanh),
                              r=[PB(b)], w=["guT"])
                for t2 in range(8):
                    b = nextbank()
                    for tt in range(2):
                        proj_tm(uvv, 0, 256, 2 * t2 + tt, b, tt * 256)
                    for tt in range(2):
                        t = 2 * t2 + tt
                        P.add("act", lambda e, b=b, t=t, tt=tt: e.activation(out=gv[:, t, :], in_=ps[:, b, tt * 256:(tt + 1) * 256], func=AF.Gelu_apprx_tanh, accum_out=s1[:, t:t + 1]),
                              r=[PB(b)], w=[("gv", t), "s1"])
                    for tt in range(2):
                        t = 2 * t2 + tt
                        P.add("dve", lambda e, t=t: e.scalar_tensor_tensor(out=jk, in0=gv[:, t, :], scalar=1.0, in1=gv[:, t, :], op0=ALU.mult, op1=ALU.mult,
                                                                          accum_out=s2[:, t:t + 1]),
                              r=[("gv", t)], w=["s2", "jk"])
                for u in (uu, uvv):
                    done_unit(u)
                P.add("dve", lambda e: e.tensor_scalar(out=mu, in0=s1, scalar1=1.0 / 256.0, scalar2=None, op0=ALU.mult), r=["s1"], w=["mu"])
                P.add("dve", lambda e: e.tensor_tensor(out=rsd, in0=mu, in1=mu, op=ALU.mult), r=["mu"], w=["rsd"])
                P.add("dve", lambda e: e.scalar_tensor_tensor(out=rsd, in0=s2, scalar=1.0 / 256.0, in1=rsd, op0=ALU.mult, op1=ALU.subtract), r=["s2", "rsd"], w=["rsd"])
                P.add("act", lambda e: e.activation(out=rsd, in_=rsd, func=AF.Sqrt, bias=cst[:, 1:2], scale=1.0), r=["rsd", "cst"], w=["rsd"])
                P.add("dve", lambda e: e.reciprocal(out=rsd, in_=rsd), r=["rsd"], w=["rsd"])
                for t in range(NT):
                    P.add("dve", lambda e, t=t: e.tensor_scalar(out=vn[:, t, :], in0=gv[:, t, :], scalar1=mu[:, t:t + 1], scalar2=rsd[:, t:t + 1], op0=ALU.subtract, op1=ALU.mult),
                          r=[("gv", t), "mu", "rsd"], w=[("vn", t)])
                bw = nextbank()
                for hd in range(4):
                    P.add("pe", lambda e, hd=hd, bw=bw: e.matmul(ps[64 * (hd % 2):64 * (hd % 2) + 64, bw, (hd // 2) * 128:(hd // 2) * 128 + 128], lhsT=ones_bf[:, 0:64], rhs=wsT[:, hd, :],
                                                                 start=True, stop=True, skip_group_check=True, tile_position=(0, 64 * (hd % 2))),
                          r=["ones", "wsT", "BAR"], w=[PB(bw)])
                for cc in range(2):
                    for q in range(4):
                        P.add("dve", lambda e, cc=cc, bw=bw, q=q: e.scalar_tensor_tensor(out=bias2[:, cc, q, :], in0=ps[:, bw, cc * 128:(cc + 1) * 128], scalar=pv(10 + cc), in1=bsb[:, cc, :],
                                                                                        op0=ALU.mult, op1=ALU.add),
                              r=[PB(bw), "pvec", "bsb"], w=["bias2"])
                for cc in range(2):
                    for n in range(4):
                        b = nextbank()
                        for q in range(4):
                            tq = 4 * n + q
                            for h2 in range(2):
                                hd = 2 * cc + h2
                                P.add("pe", lambda e, b=b, q=q, tq=tq, h2=h2, hd=hd: e.matmul(ps[64 * h2:64 * h2 + 64, b, q * 128:(q + 1) * 128], lhsT=vn[:, tq, hd * 64:(hd + 1) * 64], rhs=wsT[:, hd, :],
                                                                                             start=True, stop=True, skip_group_check=True, tile_position=(0, 64 * h2)),
                                      r=[("vn", tq), "wsT"], w=[PB(b)])
                        tm = tmx[n % 2]
                        P.add("dve", lambda e, b=b, cc=cc, tm=tm: e.scalar_tensor_tensor(out=tm, in0=ps[:, b, :], scalar=pv(8 + cc),
                                                                                        in1=bias2[:, cc].rearrange("p q n -> p (q n)"), op0=ALU.mult, op1=ALU.add),
                              r=[PB(b), "pvec", "bias2"], w=[("tm", n % 2)])
                        P.add("dve", lambda e, cc=cc, n=n, tm=tm: e.tensor_tensor(out=tm, in0=tm, in1=guT[:, cc, n * 512:(n + 1) * 512], op=ALU.mult),
                              r=[("tm", n % 2), "guT"], w=[("tm", n % 2)])
                        b2 = nextbank()
                        proj_fm(ugd, cc * 128, n, b2)
                        gs = gsm[n % 2]
                        P.add("act", lambda e, b2=b2, gs=gs: e.activation(out=gs, in_=ps[:, b2, :], func=AF.Silu),
                              r=[PB(b2)], w=[("gsm", n % 2)])
                        P.add("pool", lambda e, cc=cc, n=n, tm=tm, gs=gs: e.tensor_tensor(out=yT[:, 6 + cc, n * 512:(n + 1) * 512], in0=tm, in1=gs, op=ALU.mult),
                              r=[("tm", n % 2), ("gsm", n % 2)], w=["yT"])
                done_unit(ugd)
                if dbg and li == 0:
                    P.add("sp", lambda e: e.dma_start(out=dbg_y, in_=yT[:]), r=["yT"], dma="dbg")
                P.barrier(scratch)

                uo = ub + 12
                AR.reset()
                last = (li == depth - 1)
                if not last:
                    nrm = norm_setup(normg_d[l + 1:l + 2, :])
                elif final_norm:
                    nrm = norm_setup(fg_d)
                for t in range(NT):
                    for half in range(2):
                        b = 4 + (2 * t + half) % 4
                        slot = (uo + 2 * half) % NSLOT
                        wv = wo_view(slot)
                        for yc in range(8):
                            P.add("pe", lambda e, b=b, wv=wv, yc=yc: e.matmul(ps[:, b, :], lhsT=yT[:, yc, t * 128:(t + 1) * 128], rhs=wv[:, yc, :],
                                                                             start=(yc == 0), stop=(yc == 7)),
                                  r=["yT", ("W", slot), ("W", slot + 1)], w=[PB(b)])
                        P.add("dve", lambda e, b=b, half=half: e.tensor_tensor(out=xres[:, t, half * 512:(half + 1) * 512], in0=ps[:, b, :], in1=xres[:, t, half * 512:(half + 1) * 512], op=ALU.add),
                              r=[PB(b), ("x", t)], w=[("x", t)])
                    if not last:
                        norm_stats_act(nrm, t)
                        if t >= 1:
                            norm_stats_dve(nrm, t - 1)
                            normA_h(nrm, t - 1)
                        if t % 4 == 1 and t >= 5:
                            for t2 in range(t - 5, t - 1):
                                normA_T(nrm, t2)
                    elif final_norm:
                        norm_stats(nrm, t)
                        if t >= 1:
                            final_tile(nrm, t - 1)
                if last and final_norm:
                    final_tile(nrm, NT - 1)
                if not last:
                    norm_stats_dve(nrm, NT - 1)
                    normA_h(nrm, NT - 1)
                    for t2 in range(NT - 4, NT):
                        normA_T(nrm, t2)
                for q in range(4):
                    done_unit(uo + q)

            if not final_norm:
                for t in range(NT):
                    P.add("sp", lambda e, t=t: e.dma_start(out=out_d[t * 128:(t + 1) * 128, :], in_=xres[:, t, :]), r=[("x", t)], dma=f"o{t % 2}")
        _run()
    return nc


def _na_tables(rpb):
    p = np.arange(128)
    hlf, kc = p // 64, p % 64
    tab = np.zeros((DEPTH, 8, 128, TABW), np.float32)
    mask = np.full((128, TABW), NEG, np.float32)

    def fill(col0, nrows, kr_of_half, r_of_ri, need_edge_rule):
        q = np.arange(nrows * 64)
        ri, qc = q // 64, q % 64
        cs = np.clip(qc - 8, 0, 48)
        col_ok = (kc[:, None] >= cs[None, :]) & (kc[:, None] < cs[None, :] + 16)
        dc = np.clip(kc[:, None] - qc[None, :], -15, 15) + 15
        kr = kr_of_half(hlf)[:, None]
        r = r_of_ri(ri)[None, :]
        valid = col_ok & np.ones_like(kr + r, bool)
        if need_edge_rule:
            rs = np.clip(r - 4, 0, 24)
            inwin = (r >= kr - 3) & (r <= kr + 4)
            valid = valid & (kr >= rs) & (kr <= rs + 7) & (~inwin)
        drr = np.clip(kr - r, -7, 7) + 7
        g = rpb[:, :, drr, dc]
        tab[:, :, :, col0:col0 + nrows * 64] = np.where(valid[None, None], g, 0.0)
        mask[:, col0:col0 + nrows * 64] = np.where(valid, 0.0, NEG)

    q = np.arange(512)
    ri, qc = q // 64, q % 64
    cs = np.clip(qc - 8, 0, 48)
    col_ok = (kc[:, None] >= cs[None, :]) & (kc[:, None] < cs[None, :] + 16)
    dc = np.clip(kc[:, None] - qc[None, :], -15, 15) + 15
    drr = np.broadcast_to((3 - ri)[None, :] + 7, (128, 512))
    g = rpb[:, :, drr, dc]
    tab[:, :, :, 0:512] = np.where(col_ok[None, None], g, 0.0)
    mask[:, 0:512] = np.where(col_ok, 0.0, NEG)
    for j, col0 in ((2, 512), (3, 768), (12, 1024), (13, 1280)):
        base_r = 0 if j < 8 else 28
        fill(col0, 4, lambda h, j=j: 2 * j + h, lambda ri, base_r=base_r: base_r + ri, True)
    return tab, mask


def _pvec(conv_dw_w, conv_dw_b, conv_ln_g, conv_ln_b, pool_scale, sgu_ln_g, sgu_ln_b):
    pv = np.zeros((128, DEPTH * NVL + 34), np.float32)
    pv[:, DEPTH * NVL + 32] = ((np.arange(128) // 32) % 2 == 0)
    pv[:, DEPTH * NVL + 33] = ((np.arange(128) // 32) % 2 == 1)
    col = lambda v: v.reshape(2, 128).T
    for l in range(DEPTH):
        b = l * NVL
        pv[:, b + 0:b + 2] = col(conv_dw_b[l])
        pv[:, b + 2:b + 4] = col(conv_ln_g[l])
        pv[:, b + 4:b + 6] = col(conv_ln_b[l])
        pv[:, b + 6:b + 8] = col(pool_scale[l])
        pv[:, b + 8:b + 10] = col(sgu_ln_g[l])
        pv[:, b + 10:b + 12] = col(sgu_ln_b[l])
        w = conv_dw_w[l]
        wz = np.concatenate([w, np.zeros((1, 256), np.float32)], 0)
        for cc in range(2):
            for a in range(2):
                ch = cc * 128 + 64 * a + np.arange(64)
                for j in range(16):
                    v = np.zeros(128, np.float32)
                    first, second = wz[2 * j, ch], wz[2 * j + 1, ch]
                    if a == 0:
                        v[0:64], v[64:128] = first, second
                    else:
                        v[64:128], v[0:64] = first, second
                    pv[:, b + 12 + (cc * 2 + a) * 16 + j] = v
    for cc in range(2):
        for p in range(128):
            w = POOL_W[2 * cc + p // 64]
            for i, t in enumerate(list(range(8)) + list(range(S - 8, S))):
                lo = min(max(t - w // 2, 0), S)
                hi = min(max(t - w // 2 + w, 0), S)
                pv[p, DEPTH * NVL + cc * 16 + i] = w / (hi - lo)
    return pv


_NC_CACHE = {}


def _get_nc(key, **kw):
    if key not in _NC_CACHE:
        _NC_CACHE[key] = build(**kw)
    return _NC_CACHE[key]


def _host_inputs(x, norm_g, w_in, na_rpb, conv_dw_w, conv_dw_b, conv_ln_g, conv_ln_b, conv_pw,
                 pool_w, pool_scale, sgu_ln_g, sgu_ln_b, sgu_w, sgu_b, w_out, final_g):
    f = lambda a: np.ascontiguousarray(np.asarray(a, dtype=np.float32))
    tab, mask = _na_tables(f(na_rpb))
    sgu_wT = np.ascontiguousarray(f(sgu_w).transpose(0, 3, 1, 2))
    sb = f(sgu_b)
    sgu_bb = np.ascontiguousarray(np.repeat(sb.reshape(DEPTH, 2, 2, 1, 128), 64, axis=3)
                                  .reshape(DEPTH, 2, 128, 128).transpose(0, 2, 1, 3))
    shared = {
        "norm_g": f(norm_g), "w_in": f(w_in), "w_out": f(w_out), "final_g": f(final_g).reshape(1, D),
        "na_tab": tab, "na_mask": mask,
        "pvec": _pvec(f(conv_dw_w), f(conv_dw_b), f(conv_ln_g), f(conv_ln_b), f(pool_scale), f(sgu_ln_g), f(sgu_ln_b)),
        "conv_pw": f(conv_pw), "pool_w": f(pool_w), "sgu_wT": sgu_wT, "sgu_bb": sgu_bb,
        "ident": np.eye(128, dtype=np.float32).astype(ml_dtypes.bfloat16),
    }
    return shared


def kernel(**inputs):
    x = np.ascontiguousarray(np.asarray(inputs["x"], dtype=np.float32))
    shared = _host_inputs(**inputs)
    nc = _get_nc("full", depth=DEPTH, first_layer=0, final_norm=True)
    in_maps = [dict(shared, x=x[b]) for b in range(NCORES)]
    res = run_bass_kernel_spmd(nc, in_maps, core_ids=list(range(NCORES)))
    return np.stack([np.asarray(r["out"], dtype=np.float32) for r in res.results], axis=0)
```

```python
import contextlib
import numpy as np
import ml_dtypes
import concourse.bass as bass
import concourse.mybir as mybir
from concourse.bass_utils import run_bass_kernel_spmd

F32 = mybir.dt.float32
BF16 = mybir.dt.bfloat16
AF = mybir.ActivationFunctionType
ALU = mybir.AluOpType
AX = mybir.AxisListType

S, D, DEPTH, NCORES = 2048, 1024, 2, 8
NT = 16
DIN = 3072
NVL = 76
NEG = -30000.0
NBLK = 11
TABW = 512 + 4 * 256
SPECIAL = {2: [(0, 5)], 3: [(0, 6), (1, 7)], 12: [(14, 8), (15, 9)], 13: [(15, 10)]}
BLOCKS = [(8, 6), (8, 7), (8, 8), (8, 9), (8, 10), (2, 0), (3, 0), (3, 1), (12, 14), (12, 15), (13, 15)]
POOL_W = (2, 4, 8, 16)


class Prog:
    def __init__(self, nc, plan=None):
        self.nc = nc
        self.plan = plan
        self.engs = {"pe": nc.tensor, "act": nc.scalar, "dve": nc.vector,
                     "pool": nc.gpsimd, "sp": nc.sync}
        self.n = 0
        self.info = []
        self.cnt = []
        self.needed = []
        self.lastw = {}
        self.readers = {}
        self.last_on = {}
        self.bar = None
        self.sems = {}
        self.counts = {}
        self.waited = {}

    def sem(self, key):
        if key not in self.sems:
            self.sems[key] = self.nc.alloc_semaphore("s_" + "_".join(str(k) for k in key))
        return self.sems[key]

    def add(self, eng, fn, r=(), w=(), dma=None, extra=(), force=()):
        idx = self.n
        self.n += 1
        hard = set(extra) | set(force)
        soft = set()
        if self.bar is not None:
            hard.add(self.bar)
        for t in r:
            lw = self.lastw.get(t)
            if lw is not None:
                hard.add(lw)
        for t in w:
            lw = self.lastw.get(t)
            if lw is not None:
                hard.add(lw)
            for rd in self.readers.get(t, ()):
                soft.add(rd)
        for t in w:
            self.lastw[t] = idx
            self.readers[t] = []
        for t in r:
            self.readers.setdefault(t, []).append(idx)
        keep = []
        for d in hard | soft:
            deng, ddma = self.info[d]
            if ddma is None and dma is None and deng == eng:
                if eng == "pe" and d not in force:
                    continue
            keep.append(d)
        self.info.append((eng, dma))
        self.needed.append(dma is not None)
        if dma is None:
            self.last_on[eng] = idx
        if self.plan is None:
            for d in keep:
                self.needed[d] = True
            self.cnt.append(None)
            return idx
        E = self.engs[eng]
        need = {}
        for d in keep:
            deng, ddma = self.info[d]
            key = ("d", ddma) if ddma is not None else ("e", deng)
            need[key] = max(need.get(key, 0), self.cnt[d])
        for key, val in need.items():
            wk = (eng, key)
            if self.waited.get(wk, 0) >= val:
                continue
            self.waited[wk] = val
            E.wait_ge(self.sem(key), val)
        ins = fn(E)
        if dma is not None:
            key = ("d", dma)
            self.counts[key] = self.counts.get(key, 0) + 16
            self.cnt.append(self.counts[key])
            ins.then_inc(self.sem(key), 16)
        elif self.plan.needed[idx]:
            key = ("e", eng)
            self.counts[key] = self.counts.get(key, 0) + 1
            self.cnt.append(self.counts[key])
            ins.then_inc(self.sem(key), 1)
        else:
            self.cnt.append(None)
        return idx

    def barrier(self, scratch):
        lasts = list(self.last_on.values())
        self.bar = None
        self.bar = self.add("dve", lambda e: e.memset(scratch, 0.0), extra=lasts, w=["BAR"])

    def finish(self, eng="sp"):
        E = self.engs[eng]
        for key, val in self.counts.items():
            if self.waited.get((eng, key), 0) < val:
                E.wait_ge(self.sem(key), val)


class Arena:
    def __init__(self, ap_f32, nwords):
        self.base, self.n, self.off = ap_f32, nwords, 0

    def alloc(self, cols, dt=F32):
        words = cols if dt == F32 else (cols + 1) // 2
        assert self.off + words <= self.n, ("arena overflow", self.off, words, self.n)
        v = self.base[:, self.off:self.off + words]
        self.off += words
        return v if dt == F32 else v.bitcast(BF16)

    def reset(self, off=0):
        self.off = off


def blocks(j):
    lo, hi = max(0, j - 2), min(15, j + 2)
    sp = SPECIAL.get(j, [])
    spm = {m for m, _ in sp}
    while lo in spm:
        lo += 1
    while hi in spm:
        hi -= 1
    return lo, hi, sp


def complete_after(m):
    return 3 if m == 0 else min(15, m + 2)


def build(depth=DEPTH, first_layer=0, final_norm=True, dbg=False):
    nc = bass.Bass("TRN2", target_bir_lowering=False)
    dr = lambda name, shape, dt=F32: nc.dram_tensor(name, shape, dt, kind="ExternalInput").ap()
    x_d = dr("x", [S, D])
    normg_d = dr("norm_g", [DEPTH, D])
    win_d = dr("w_in", [DEPTH, D, DIN])
    wout_d = dr("w_out", [DEPTH, D, D])
    fg_d = dr("final_g", [1, D])
    tab_d = dr("na_tab", [DEPTH, 8, 128, TABW])
    mask_d = dr("na_mask", [128, TABW])
    pvec_d = dr("pvec", [128, DEPTH * NVL + 34])
    pw_d = dr("conv_pw", [DEPTH, 256, 256])
    poolw_d = dr("pool_w", [DEPTH, 4, 64, 64])
    swt_d = dr("sgu_wT", [DEPTH, 128, 4, 128])
    sbb_d = dr("sgu_bb", [DEPTH, 128, 2, 128])
    ident_d = dr("ident", [128, 128], BF16)
    out_d = nc.dram_tensor("out", [S, D], F32, kind="ExternalOutput").ap()
    if dbg:
        dbg_y = nc.dram_tensor("dbg_y", [128, 8, S], BF16, kind="ExternalOutput").ap()
        dbg_h = nc.dram_tensor("dbg_h", [128, 8, S], BF16, kind="ExternalOutput").ap()

    st = contextlib.ExitStack()
    with st:
        sb = lambda name, shape, dt: st.enter_context(nc.sbuf_tensor(name, shape, dt))
        xres = sb("xres", [128, NT, D], F32)
        hT = sb("hT", [128, 8, S], BF16)
        yT = sb("yT", [128, 8, S], BF16)
        NSLOT = 6
        Wr = sb("Wr", [128, NSLOT, 8, 256], BF16)
        ident = sb("ident_sb", [128, 128], BF16)
        ones_bf = sb("ones_bf", [128, 128], BF16)
        ident32 = sb("ident32", [128, 128], F32)
        i2f = sb("i2f", [128, 64], F32)
        i2b = sb("i2b", [128, 64], BF16)
        onesm = sb("onesm", [128, 128], BF16)
        pvec = sb("pvec_sb", [128, DEPTH * NVL + 34], F32)
        maskb = sb("maskb", [128, TABW], BF16)
        cst = sb("cst", [128, 16], F32)
        ARW = 12424
        arena_t = sb("arena", [128, ARW], F32)
        ps = st.enter_context(nc.psum_tensor("ps", [128, 8, 512], F32))
        psflat = ps[:].rearrange("p b n -> p (b n)")
        AR = Arena(arena_t[:], ARW)

        scratch = cst[:, 2:3]
        PB = lambda b: ("ps", b)

        def body(P):
            _body(P)

        def _run():
            P1 = Prog(nc, plan=None)
            body(P1)
            P2 = Prog(nc, plan=P1)
            body(P2)
            assert P1.n == P2.n
            P2.finish()

        def _body(P):
            def norm_setup(g_row):
                d = {"ss": AR.alloc(16), "rstd": AR.alloc(16), "junk": AR.alloc(1024),
                     "hts": [AR.alloc(1024, BF16) for _ in range(8)],
                     "ob": [AR.alloc(1024) for _ in range(2)], "gbc": AR.alloc(D)}
                P.add("sp", lambda e: e.dma_start(out=d["gbc"], in_=g_row.partition_broadcast(128)), w=["gbc"], dma="g")
                return d

            def norm_stats_act(d, t):
                ss, rstd = d["ss"], d["rstd"]
                P.add("act", lambda e: e.activation(out=d["junk"], in_=xres[:, t, :], func=AF.Square, accum_out=ss[:, t:t + 1]),
                      r=[("x", t)], w=[("ss", t), "junk"])
                P.add("act", lambda e: e.activation(out=rstd[:, t:t + 1], in_=ss[:, t:t + 1], func=AF.Sqrt, bias=cst[:, 0:1], scale=1.0 / D),
                      r=[("ss", t), "cst"], w=[("rstd", t)])

            def norm_stats_dve(d, t):
                rstd = d["rstd"]
                P.add("dve", lambda e: e.reciprocal(out=rstd[:, t:t + 1], in_=rstd[:, t:t + 1]), r=[("rstd", t)], w=[("rstd", t)])

            def norm_stats(d, t):
                norm_stats_act(d, t)
                norm_stats_dve(d, t)

            def normA_h(d, t):
                hb = d["hts"][t % 8]
                P.add("dve", lambda e: e.scalar_tensor_tensor(out=hb, in0=xres[:, t, :], scalar=d["rstd"][:, t:t + 1], in1=d["gbc"],
                                                              op0=ALU.mult, op1=ALU.mult),
                      r=[("x", t), ("rstd", t), "gbc"], w=[("ht", t % 8)])

            def normA_tile(d, t, after_stt=None):
                normA_h(d, t)
                if after_stt is not None:
                    after_stt()
                normA_T(d, t)

            def normA_T(d, t):
                hb = d["hts"][t % 8]
                bk = t % 4
                psb = ps[:, bk, :].bitcast(BF16)
                for c in range(8):
                    P.add("pe", lambda e, c=c: e.transpose(out=psb[:, c * 128:(c + 1) * 128], in_=hb[:, c * 128:(c + 1) * 128], identity=ident[:]),
                          r=[("ht", t % 8), "ident"], w=[PB(bk)])
                P.add("act", lambda e: e.activation(out=hT[:, :, t * 128:(t + 1) * 128],
                                                    in_=psb.rearrange("p (c n) -> p c n", c=8), func=AF.Copy),
                      r=[PB(bk)], w=["hT"])

            def final_tile(d, t):
                o = d["ob"][t % 2]
                P.add("dve", lambda e: e.scalar_tensor_tensor(out=o, in0=xres[:, t, :], scalar=d["rstd"][:, t:t + 1], in1=d["gbc"], op0=ALU.mult, op1=ALU.mult),
                      r=[("x", t), ("rstd", t), "gbc"], w=[("ob", t % 2)])
                P.add("sp", lambda e: e.dma_start(out=out_d[t * 128:(t + 1) * 128, :], in_=o), r=[("ob", t % 2)], dma=f"o{t % 2}")

            P.add("sp", lambda e: e.dma_start(out=ident[:], in_=ident_d), w=["ident"], dma="c0")
            P.add("sp", lambda e: e.dma_start(out=pvec[:], in_=pvec_d), w=["pvec"], dma="c1")
            AR.reset()
            nrm_first = norm_setup(normg_d[first_layer:first_layer + 1, :])
            xload = []
            for t in range(NT):
                xload.append(P.add("sp", lambda e, t=t: e.dma_start(out=xres[:, t, :], in_=x_d[t * 128:(t + 1) * 128, :]),
                                   w=[("x", t)], dma=f"x{t}"))
            P.add("pool", lambda e: e.dma_start(out=maskb[:], in_=mask_d), w=["maskb"], dma="c2")
            P.add("dve", lambda e: e.memset(ones_bf[:], 1.0), w=["ones"])
            P.add("dve", lambda e: e.tensor_copy(out=ident32[:], in_=ident[:]), r=["ident"], w=["ident32"])
            P.add("dve", lambda e: e.tensor_tensor(out=i2f[:], in0=ident32[:, 0:64], in1=ident32[:, 64:128], op=ALU.add), r=["ident32"], w=["i2"])
            P.add("dve", lambda e: e.tensor_copy(out=i2b[:], in_=i2f[:]), r=["i2"], w=["i2"])
            P.add("dve", lambda e: e.memset(onesm[:], 1.0 / 256.0), w=["onesm"])
            P.add("dve", lambda e: e.memset(cst[:, 0:1], 1e-6), w=["cst"])
            P.add("dve", lambda e: e.memset(cst[:, 1:2], 1e-5), w=["cst"])

            units = []
            for l in range(first_layer, first_layer + depth):
                for c0 in (0, 256, 512, 2048,
                           768, 1024, 2304,
                           1280, 2560,
                           1536, 1792, 2816):
                    units.append(("in", l, c0))
                for c0 in (0, 256, 512, 768):
                    units.append(("out", l, c0))
            UPL = 16
            issued = [0]

            def load_unit(i):
                kind, l, c0 = units[i]
                slot = i % NSLOT
                if kind == "in":
                    src = win_d[l, :, c0:c0 + 256].rearrange("(c p) n -> p c n", p=128)
                    P.add("pool", lambda e: e.dma_start(out=Wr[:, slot], in_=src), w=[("W", slot)], dma=f"W{slot}",
                          extra=([xload[11]] if i < 4 else []))
                elif c0 % 512 == 256:
                    s0 = slot - 1
                    assert s0 % 2 == 0 and s0 >= 0
                    src = wout_d[l, :, c0 - 256:c0 + 256].rearrange("(c p) n -> p c n", p=128)
                    P.add("pool", lambda e: e.dma_start(out=wo_view(s0), in_=src), w=[("W", s0), ("W", s0 + 1)], dma=f"W{s0}")

            def wo_view(slot):
                return Wr[:, slot:slot + 2].rearrange("p s c n -> p (s c n)").rearrange("p (c m) -> p c m", c=8)

            def done_unit(i):
                nxt = i + NSLOT
                while issued[0] <= nxt and issued[0] < len(units):
                    load_unit(issued[0])
                    issued[0] += 1

            for i in range(4):
                load_unit(i)
            issued[0] = 4

            bank_rr = [0]

            def nextbank():
                b = bank_rr[0]
                bank_rr[0] = (b + 1) % 8
                return b

            def proj_fm(ui, col, n, bank):
                slot = ui % NSLOT
                for c in range(8):
                    P.add("pe", lambda e, c=c: e.matmul(ps[:, bank, :], lhsT=Wr[:, slot, c, col:col + 128],
                                                        rhs=hT[:, c, n * 512:(n + 1) * 512], start=(c == 0), stop=(c == 7)),
                          r=[("W", slot), "hT", "BAR"], w=[PB(bank)])

            def proj_tm(ui, col, ncol, t, bank, off):
                slot = ui % NSLOT
                for c in range(8):
                    P.add("pe", lambda e, c=c: e.matmul(ps[:, bank, off:off + ncol], lhsT=hT[:, c, t * 128:(t + 1) * 128],
                                                        rhs=Wr[:, slot, c, col:col + ncol], start=(c == 0), stop=(c == 7)),
                          r=[("W", slot), "hT", "BAR"], w=[PB(bank)])

            for li in range(depth):
                l = first_layer + li
                ub = li * UPL
                pv = lambda k: pvec[:, l * NVL + k:l * NVL + k + 1]

                if li == 0:
                    nrm = nrm_first
                    norm_stats(nrm, 0)
                    norm_stats(nrm, 1)
                    for t in range(NT):
                        if t + 2 < NT:
                            norm_stats_act(nrm, t + 2)
                            normA_tile(nrm, t, after_stt=lambda t=t: norm_stats_dve(nrm, t + 2))
                        else:
                            normA_tile(nrm, t)
                if dbg and li == 0:
                    P.add("sp", lambda e: e.dma_start(out=dbg_h, in_=hT[:]), r=["hT"], dma="dbg")
                if li == 0:
                    while issued[0] < min(NSLOT, len(units)):
                        load_unit(issued[0])
                        issued[0] += 1
                P.barrier(scratch)

                uq, uk, uv, ug = ub + 0, ub + 1, ub + 2, ub + 3
                for g in range(2):
                    AR.reset()
                    qT = AR.alloc(S, BF16)
                    kTm = [AR.alloc(S, BF16) for _ in range(2)]
                    gT = AR.alloc(S, BF16)
                    vtok = AR.alloc(NT * 128, BF16).rearrange("p (t n) -> p t n", t=NT)
                    E = AR.alloc(4 * TABW, BF16).rearrange("p (h n) -> p h n", h=4)
                    NPT = 8
                    PT = [AR.alloc(512, BF16) for _ in range(NPT)]
                    tyrc = [AR.alloc(256).rearrange("p (a n) -> p a n", a=2) for _ in range(4)]
                    tyb = [t_[:, 0, :] for t_ in tyrc]
                    rcb = [t_[:, 1, :] for t_ in tyrc]
                    P.add("pool", lambda e, g=g: e.dma_start(out=E, in_=tab_d[l, 4 * g:4 * g + 4].rearrange("h p n -> p h n")),
                          r=["BAR"], w=["E"] + [("E", hh_) for hh_ in range(4)], dma="E")
                    def e_prep(hh):
                        P.add("dve", lambda e: e.tensor_tensor(out=E[:, hh, :], in0=E[:, hh, :], in1=maskb[:], op=ALU.add),
                              r=["E", "maskb"], w=[("E", hh)])
                        P.add("act", lambda e: e.activation(out=E[:, hh, :], in_=E[:, hh, :], func=AF.Exp),
                              r=[("E", hh)], w=[("E", hh)])

                    for n in range(4):
                        if n >= 2:
                            e_prep(2 * (n - 2))
                            e_prep(2 * (n - 2) + 1)
                        b = nextbank()
                        proj_fm(uq, g * 128, n, b)
                        P.add("dve", lambda e, n=n, b=b: e.tensor_scalar(out=qT[:, n * 512:(n + 1) * 512], in0=ps[:, b, :], scalar1=32.0 ** -0.5, scalar2=None, op0=ALU.mult),
                              r=[PB(b)], w=["qT"])
                        b = nextbank()
                        proj_fm(uk, g * 128, n, b)
                        for par in range(2):
                            mcol = pvec[:, DEPTH * NVL + 32 + par:DEPTH * NVL + 33 + par]
                            P.add("act", lambda e, n=n, b=b, par=par, mcol=mcol: e.activation(out=kTm[par][:, n * 512:(n + 1) * 512], in_=ps[:, b, :], func=AF.Identity, scale=mcol),
                                  r=[PB(b), "pvec"], w=["kT"])
                        b = nextbank()
                        proj_fm(ug, g * 128, n, b)
                        P.add("act", lambda e, n=n, b=b: e.activation(out=gT[:, n * 512:(n + 1) * 512], in_=ps[:, b, :], func=AF.Silu),
                              r=[PB(b)], w=["gT"])
                    for t4 in range(4):
                        b = nextbank()
                        for tt in range(4):
                            proj_tm(uv, g * 128, 128, 4 * t4 + tt, b, tt * 128)
                        P.add("act", lambda e, t4=t4, b=b: e.activation(out=vtok[:, 4 * t4:4 * t4 + 4, :], in_=ps[:, b, :].rearrange("p (t n) -> p t n", t=4), func=AF.Copy),
                              r=[PB(b)], w=["vtok"])
                    if g == 1:
                        for u in (uq, uk, uv, ug):
                            done_unit(u)
                    for b in range(4, 8):
                        P.add("dve", lambda e, b=b: e.memset(ps[:, b, :], 0.0), w=[PB(b)])
                    accy = psflat[:, 4 * 512:6 * 512]
                    accs = psflat[:, 6 * 512:8 * 512]
                    acc2 = psflat[:, 4 * 512:8 * 512].rearrange("p (a n) -> p a n", a=2)
                    ACCY = lambda m: PB(4 + (m % 8) // 4)
                    ACCS = lambda m: PB(6 + (m % 8) // 4)
                    MULENG = {0: "dve", 1: "dve", 2: "dve", 3: "dve"}
                    seq = []
                    for j in range(16):
                        seq.append(("reg", j))
                        if j == 3:
                            seq.append(("xt", 0))
                        if j == 13:
                            seq.append(("xb", 0))
                    pos_of = {kt: i for i, kt in enumerate(seq)}

                    def parts_of(kind, j):
                        out = []
                        if kind == "reg":
                            for h in range(2):
                                kr = 2 * j + h
                                lo_, hi_ = max(0, kr - 3), min(31, kr + 4)
                                out.append((j, h, lo_, hi_, (lo_ - (kr - 3)) * 64))
                            return out, 0
                        jks, rows, t0 = ((2, 3), (0, 3), 512) if kind == "xt" else ((12, 13), (28, 31), 1024)
                        for k, jk in enumerate(jks):
                            for h in range(2):
                                out.append((jk, h, rows[0], rows[1], 256 * k))
                        return out, t0

                    last_pos = {}
                    for m in range(16):
                        cand = [pos_of[("reg", j)] for j in range(16) if max(0, 2 * j - 3) <= 2 * m + 1 and min(31, 2 * j + 5) >= 2 * m]
                        if m <= 1:
                            cand.append(pos_of[("xt", 0)])
                        if m >= 14:
                            cand.append(pos_of[("xb", 0)])
                        last_pos[m] = max(cand)

                    def s_burst(i):
                        kind, j = seq[i]
                        pts, t0 = parts_of(kind, j)
                        c_lo = min(c0 for (_, _, _, _, c0) in pts)
                        c_hi = max(c0 + (hi_ - lo_ + 1) * 64 for (_, _, lo_, hi_, c0) in pts)
                        for par in range(2):
                            for (jk, h, lo_, hi_, c0) in pts:
                                n_ = (hi_ - lo_ + 1) * 64
                                for sb_ in range(2):
                                    for g2 in range(2):
                                        hh = 2 * g2 + par
                                        rs_ = slice(64 * g2, 64 * g2 + 64)
                                        k0 = jk * 128 + 64 * h + 32 * sb_
                                        o0 = 64 * h + 32 * sb_
                                        P.add("pe", lambda e, hh=hh, rs_=rs_, k0=k0, o0=o0, lo_=lo_, c0=c0, n_=n_, par=par, g2=g2: e.matmul(
                                            ps[o0:o0 + 32, hh, c0:c0 + n_], lhsT=kTm[par][rs_, k0:k0 + 32], rhs=qT[rs_, lo_ * 64:lo_ * 64 + n_],
                                            start=True, stop=True, skip_group_check=True, tile_position=(64 * g2, o0)),
                                            r=["qT", "kT"], w=[PB(hh)])
                        for hh in range(4):
                            pt = PT[(4 * i + hh) % NPT]
                            ptk = ("PT", (4 * i + hh) % NPT)
                            P.add("act", lambda e, hh=hh, pt=pt: e.activation(out=pt[:, c_lo:c_hi], in_=ps[:, hh, c_lo:c_hi], func=AF.Exp),
                                  r=[PB(hh)], w=[ptk])
                            P.add(MULENG[hh], lambda e, hh=hh, pt=pt: e.tensor_tensor(out=pt[:, c_lo:c_hi], in0=pt[:, c_lo:c_hi], in1=E[:, hh, t0 + c_lo:t0 + c_hi], op=ALU.mult),
                                  r=[ptk, ("E", hh)], w=[ptk])

                    pvlast = [None]

                    def pv_block(i):
                        kind, j = seq[i]
                        pts, _t0 = parts_of(kind, j)
                        items = {0: {0: [], 1: []}, 1: {0: [], 1: []}}
                        for (jk, h, lo_, hi_, c0) in pts:
                            r0 = lo_
                            while r0 <= hi_:
                                r1 = min(hi_, (r0 // 8) * 8 + 7)
                                for which in range(2):
                                    items[h][which].append((jk, lo_, c0, r0, r1))
                                r0 = r1 + 1

                        def group(h, which, jk, lo_, c0, r0, r1, frc):
                            wdt = (r1 - r0 + 1) * 64
                            oc = (r0 % 16) * 64
                            pc = c0 + (r0 - lo_) * 64
                            bank = (4 if which == 0 else 6) + (r0 % 16) // 8
                            for hh in range(4):
                                pt = PT[(4 * i + hh) % NPT]
                                hs = slice(32 * hh, 32 * hh + 32)
                                acc = accy if which == 0 else accs
                                lhs = vtok[64 * h:64 * h + 64, jk, hs] if which == 0 else ones_bf[64 * h:64 * h + 64, 0:32]
                                pvlast[0] = P.add("pe", lambda e, acc=acc, lhs=lhs, pt=pt, hs=hs, hh=hh: e.matmul(
                                    acc[hs, oc:oc + wdt], lhsT=lhs, rhs=pt[64 * h:64 * h + 64, pc:pc + wdt],
                                    start=False, stop=True, skip_group_check=True, tile_position=(64 * h, 32 * hh)),
                                    r=[("PT", (4 * i + hh) % NPT), "vtok", "ones"], w=[PB(bank)], force=(frc if hh == 0 else []))
                                frc = []

                        for phase in range(2):
                            l0 = items[0][phase]
                            l1 = items[1][1 - phase]
                            frc = [pvlast[0]] if (phase == 1 and pvlast[0] is not None) else []
                            for k in range(max(len(l0), len(l1))):
                                if k < len(l0):
                                    group(0, phase, *l0[k], frc)
                                    frc = []
                                if k < len(l1):
                                    group(1, 1 - phase, *l1[k], frc)
                                    frc = []
                        for m in range(16):
                            if last_pos[m] != i:
                                continue
                            oc = (m % 8) * 128
                            rc, ty = rcb[m % 4], tyb[m % 4]
                            P.add("dve", lambda e, m=m, oc=oc: e.tensor_copy(out=tyrc[m % 4], in_=acc2[:, :, oc:oc + 128]),
                                  r=[ACCY(m), ACCS(m)], w=[("ty", m % 4), ("rc", m % 4)])
                            P.add("dve", lambda e, oc=oc: e.memset(acc2[:, :, oc:oc + 128], 0.0),
                                  r=[ACCY(m), ACCS(m)], w=[ACCY(m), ACCS(m)])
                            def post(rc=rc, ty=ty, m=m):
                                P.add("act", lambda e: e.activation(out=rc, in_=rc, func=AF.Ln), r=[("rc", m % 4)], w=[("rc", m % 4)])
                                P.add("act", lambda e: e.activation(out=rc, in_=rc, func=AF.Exp, scale=-1.0), r=[("rc", m % 4)], w=[("rc", m % 4)])
                                P.add("dve", lambda e: e.tensor_tensor(out=ty, in0=ty, in1=rc, op=ALU.mult),
                                      r=[("ty", m % 4), ("rc", m % 4)], w=[("ty", m % 4)])
                                P.add("dve", lambda e: e.tensor_tensor(out=yT[:, g, m * 128:(m + 1) * 128], in0=ty, in1=gT[:, m * 128:(m + 1) * 128], op=ALU.mult),
                                      r=[("ty", m % 4), "gT"], w=["yT"])
                            posts.append(post)

                    posts = []
                    for i in range(len(seq)):
                        s_burst(i)
                        for p_ in posts:
                            p_()
                        del posts[:]
                        if i > 0:
                            pv_block(i - 1)
                    pv_block(len(seq) - 1)
                    for p_ in posts:
                        p_()
                    del posts[:]
                    if g == 1:
                        P.barrier(scratch)

                ua, ubb, ugb = ub + 4, ub + 5, ub + 6
                AR.reset()
                GLW = S + 30
                gl = AR.alloc(2 * GLW, BF16).rearrange("p (c n) -> p c n", c=2)
                gl0 = AR.alloc(2 * GLW, BF16).rearrange("p (c n) -> p c n", c=2)
                dg = AR.alloc(64 * 64, BF16).rearrange("p (k n) -> p k n", k=64)
                PWs = AR.alloc(2 * 256, BF16).rearrange("p (c n) -> p c n", c=2)
                sig = [AR.alloc(512) for _ in range(2)]
                c32s = [AR.alloc(2 * 512).rearrange("p (c n) -> p c n", c=2) for _ in range(2)]
                cbf = AR.alloc(2 * 512, BF16).rearrange("p (c n) -> p c n", c=2)
                sqb = AR.alloc(2 * 512, BF16).rearrange("p (c n) -> p c n", c=2)
                mean = AR.alloc(512)
                msq = AR.alloc(512)
                rs = msq
                hsb = AR.alloc(2 * 512, BF16).rearrange("p (c n) -> p c n", c=2)
                P.add("pool", lambda e: e.dma_start(out=PWs, in_=pw_d[l].rearrange("(c p) n -> p c n", p=128)), r=["BAR"], w=["PWs"], dma="pw")
                P.add("pool", lambda e: e.memset(gl[:, :, 0:15], 0.0), r=["BAR"], w=["glpad"])
                P.add("pool", lambda e: e.memset(gl[:, :, 15 + S:GLW], 0.0), r=["BAR"], w=["glpad"])
                P.add("pool", lambda e: e.memset(gl0[:, :, GLW - 1:GLW], 0.0), r=["BAR"], w=["gl0pad", "gl0"])
                dgi = [0]

                def diag_some(k):
                    for _ in range(k):
                        i = dgi[0]
                        if i >= 64:
                            return
                        dgi[0] += 1
                        if i % 2 == 0:
                            P.add("act", lambda e, i=i: e.activation(out=dg[:, i, :], in_=i2b[:], func=AF.Identity, scale=pv(12 + i)),
                                  r=["i2", "pvec"], w=["dg"])
                        else:
                            P.add("dve", lambda e, i=i: e.tensor_scalar(out=dg[:, i, :], in0=i2f[:], scalar1=pv(12 + i), scalar2=None, op0=ALU.mult),
                                  r=["i2", "pvec"], w=["dg"])

                for cc in range(2):
                    for n in range(4):
                        b = nextbank()
                        proj_fm(ubb, cc * 128, n, b)
                        sg = sig[n % 2]
                        P.add("act", lambda e, b=b, sg=sg: e.activation(out=sg, in_=ps[:, b, :], func=AF.Sigmoid),
                              r=[PB(b)], w=[("sig", n % 2)])
                        b = nextbank()
                        proj_fm(ua, cc * 128, n, b)
                        P.add("dve", lambda e, b=b, sg=sg, cc=cc, n=n: e.tensor_tensor(out=gl[:, cc, 15 + n * 512:15 + (n + 1) * 512], in0=ps[:, b, :], in1=sg, op=ALU.mult),
                              r=[PB(b), ("sig", n % 2)], w=["gl"])
                        diag_some(8)
                    P.add("dve", lambda e, cc=cc: e.tensor_copy(out=gl0[0:64, cc, :], in_=gl[0:64, cc, :]), r=["gl", "glpad"], w=["gl0"])
                    P.add("dve", lambda e, cc=cc: e.tensor_copy(out=gl0[64:128, cc, 0:GLW - 1], in_=gl[0:64, cc, 1:GLW]), r=["gl", "glpad"], w=["gl0"])
                    P.add("dve", lambda e, cc=cc: e.tensor_copy(out=gl[0:64, cc, 0:GLW - 1], in_=gl[64:128, cc, 1:GLW]), r=["gl", "glpad", "gl0"], w=["gl"])
                for u in (ua, ubb):
                    done_unit(u)

                def conv_stage(n):
                    c32 = c32s[n % 2]
                    cb = [nextbank(), nextbank()]
                    for cc in range(2):
                        for j in range(16):
                            for a in range(2):
                                src = gl0 if a == 0 else gl
                                P.add("pe", lambda e, cc=cc, j=j, a=a, src=src, b=cb[cc]: e.matmul(
                                    ps[64 * a:64 * a + 64, b, :], lhsT=dg[:, (cc * 2 + a) * 16 + j, :], rhs=src[:, cc, n * 512 + 2 * j:n * 512 + 2 * j + 512],
                                    start=(j == 0), stop=(j == 15), skip_group_check=True, tile_position=(0, 64 * a)),
                                    r=["dg", "gl", "gl0", "glpad", "gl0pad"], w=[PB(cb[cc])])
                        P.add("dve", lambda e, cc=cc, b=cb[cc]: e.tensor_scalar(out=c32[:, cc, :], in0=ps[:, b, :], scalar1=pv(0 + cc), scalar2=None, op0=ALU.add),
                              r=[PB(cb[cc]), "pvec"], w=[("c32", n % 2, cc)])

                sbanks = {}

                def stats_a1(n):
                    c32 = c32s[n % 2]
                    for cc in range(2):
                        P.add("act", lambda e, cc=cc: e.activation(out=cbf[:, cc, :], in_=c32[:, cc, :], func=AF.Copy), r=[("c32", n % 2, cc)], w=[("cbf", cc)])
                        P.add("act", lambda e, cc=cc: e.activation(out=sqb[:, cc, :], in_=c32[:, cc, :], func=AF.Square), r=[("c32", n % 2, cc)], w=[("sqb", cc)])

                def stats_a2(n):
                    bm, be = nextbank(), nextbank()
                    sbanks[n] = (bm, be)
                    for cc in range(2):
                        P.add("pe", lambda e, cc=cc: e.matmul(ps[:, bm, :], lhsT=onesm[:], rhs=cbf[:, cc, :], start=(cc == 0), stop=(cc == 1)),
                              r=["onesm", ("cbf", cc)], w=[PB(bm)])
                    for cc in range(2):
                        P.add("pe", lambda e, cc=cc: e.matmul(ps[:, be, :], lhsT=onesm[:], rhs=sqb[:, cc, :], start=(cc == 0), stop=(cc == 1)),
                              r=["onesm", ("sqb", cc)], w=[PB(be)])

                def stats_b(n):
                    c32 = c32s[n % 2]
                    bm, be = sbanks[n]
                    P.add("act", lambda e: e.activation(out=msq, in_=ps[:, bm, :], func=AF.Square), r=[PB(bm)], w=["msq"])
                    P.add("dve", lambda e: e.tensor_tensor(out=rs, in0=ps[:, be, :], in1=msq, op=ALU.subtract), r=[PB(be), "msq"], w=["msq"])
                    P.add("act", lambda e: e.activation(out=rs, in_=rs, func=AF.Ln, bias=cst[:, 1:2], scale=1.0), r=["msq", "cst"], w=["msq"])
                    P.add("act", lambda e: e.activation(out=rs, in_=rs, func=AF.Exp, scale=-0.5), r=["msq"], w=["msq"])
                    for cc in range(2):
                        P.add("dve", lambda e, cc=cc: e.tensor_tensor(out=c32[:, cc, :], in0=c32[:, cc, :], in1=ps[:, bm, :], op=ALU.subtract), r=[("c32", n % 2, cc), PB(bm), "msq"], w=[("c32", n % 2, cc)])
                        P.add("dve", lambda e, cc=cc: e.tensor_tensor(out=c32[:, cc, :], in0=c32[:, cc, :], in1=rs, op=ALU.mult), r=[("c32", n % 2, cc), "msq"], w=[("c32", n % 2, cc)])
                        P.add("act", lambda e, cc=cc: e.activation(out=hsb[:, cc, :], in_=c32[:, cc, :], func=AF.Silu, scale=pv(2 + cc), bias=pv(4 + cc)),
                              r=[("c32", n % 2, cc), "pvec"], w=[("hsb", cc)])

                def pw_stage(n):
                    for fo in range(2):
                        b2 = nextbank()
                        proj_fm(ugb, fo * 128, n, b2)
                        sg = sig[fo]
                        P.add("act", lambda e, b2=b2, sg=sg: e.activation(out=sg, in_=ps[:, b2, :], func=AF.Silu),
                              r=[PB(b2)], w=[("sig", fo)])
                        b = nextbank()
                        for cc in range(2):
                            P.add("pe", lambda e, cc=cc, fo=fo, b=b: e.matmul(ps[:, b, :], lhsT=PWs[:, cc, fo * 128:(fo + 1) * 128], rhs=hsb[:, cc, :], start=(cc == 0), stop=(cc == 1)),
                                  r=["PWs", ("hsb", cc)], w=[PB(b)])
                        P.add("dve", lambda e, fo=fo, b=b, sg=sg: e.tensor_tensor(out=yT[:, 2 + fo, n * 512:(n + 1) * 512], in0=ps[:, b, :], in1=sg, op=ALU.mult),
                              r=[PB(b), ("sig", fo)], w=["yT"])

                conv_stage(0)
                stats_a1(0)
                stats_a2(0)
                stats_b(0)
                for n in range(4):
                    if n + 1 < 4:
                        conv_stage(n + 1)
                        stats_a1(n + 1)
                    pw_stage(n)
                    if n + 1 < 4:
                        stats_a2(n + 1)
                        stats_b(n + 1)
                done_unit(ugb)
                P.barrier(scratch)

                up, ugc = ub + 7, ub + 8
                AR.reset()
                XW = S + 16
                gTc = AR.alloc(2 * S, BF16).rearrange("p (c n) -> p c n", c=2)
                PWt = AR.alloc(2 * 128, BF16).rearrange("p (c n) -> p c n", c=2)
                Xs = [AR.alloc(XW) for _ in range(2)]
                A_ = AR.alloc(XW)
                B_ = AR.alloc(XW)
                plb1 = AR.alloc(S, BF16)
                plbs = [plb1, Xs[0][:, 0:S // 2].bitcast(BF16)]
                P.add("pool", lambda e: e.memset(PWt, 0.0), w=["PWt"])
                for cc in range(2):
                    for hf in range(2):
                        P.add("pool", lambda e, cc=cc, hf=hf: e.dma_start(out=PWt[64 * hf:64 * hf + 64, cc, 64 * hf:64 * hf + 64], in_=poolw_d[l, 2 * cc + hf]),
                              r=["PWt"], w=["PWt"], dma="pwt")
                for cc in range(2):
                    P.add("pool", lambda e, cc=cc: e.memset(Xs[cc][:, 0:8], 0.0), w=[("Xpad", cc)])
                    P.add("pool", lambda e, cc=cc: e.memset(Xs[cc][:, 8 + S:XW], 0.0), w=[("Xpad", cc)])
                for cc in range(2):
                    for n in range(4):
                        b = nextbank()
                        proj_fm(up, cc * 128, n, b)
                        P.add("act", lambda e, b=b, n=n, cc=cc: e.activation(out=Xs[cc][:, 8 + n * 512:8 + (n + 1) * 512], in_=ps[:, b, :], func=AF.Copy),
                              r=[PB(b)], w=[("X", cc)])
                for cc in range(2):
                    for n in range(4):
                        b = nextbank()
                        proj_fm(ugc, cc * 128, n, b)
                        P.add("act", lambda e, b=b, cc=cc, n=n: e.activation(out=gTc[:, cc, n * 512:(n + 1) * 512], in_=ps[:, b, :], func=AF.Silu),
                              r=[PB(b)], w=["gTc"])
                done_unit(up)
                done_unit(ugc)
                def c_sums(cc):
                    X = Xs[cc]
                    add = lambda o, a, b_, tok_r, tok_w: P.add("dve", lambda e: e.tensor_tensor(out=o, in0=a, in1=b_, op=ALU.add), r=tok_r, w=tok_w)
                    xt = [("X", cc), ("Xpad", cc)]
                    add(A_[:, 1:XW], X[:, 0:XW - 1], X[:, 1:XW], xt, ["A"])
                    if cc == 0:
                        add(B_[64:128, 2:XW - 1], A_[64:128, 1:XW - 2], A_[64:128, 3:XW], ["A"], ["B"])
                    else:
                        add(B_[:, 2:XW - 1], A_[:, 1:XW - 2], A_[:, 3:XW], ["A"], ["B"])
                        add(A_[:, 4:XW - 3], B_[:, 2:XW - 5], B_[:, 6:XW - 1], ["B"], ["A"])
                        add(B_[64:128, 8:XW - 8], A_[64:128, 4:XW - 12], A_[64:128, 12:XW - 4], ["A"], ["B"])

                def c_pool(cc):
                    X = Xs[cc]
                    plb = plbs[cc]
                    srcs = [(A_, 2.0), (B_, 4.0)] if cc == 0 else [(A_, 8.0), (B_, 16.0)]
                    for hf in range(2):
                        src, wv = srcs[hf]
                        pr = slice(64 * hf, 64 * hf + 64)
                        for (e0, tcol) in ((8, 0), (8 + S - 8, 8)):
                            dc = DEPTH * NVL + cc * 16 + tcol
                            P.add("dve", lambda e, src=src, pr=pr, e0=e0, dc=dc: e.tensor_tensor(out=src[pr, e0:e0 + 8], in0=src[pr, e0:e0 + 8], in1=pvec[pr, dc:dc + 8], op=ALU.mult),
                                  r=["A", "B", "pvec"], w=["A", "B"])
                        P.add("dve", lambda e, src=src, pr=pr, wv=wv: e.scalar_tensor_tensor(out=plb[pr, :], in0=src[pr, 8:8 + S], scalar=1.0 / wv, in1=X[pr, 8:8 + S],
                                                                                             op0=ALU.mult, op1=ALU.subtract),
                              r=["A", "B", ("X", cc)], w=[("plb", cc)] + ([("X", 0)] if cc == 1 else []))

                def c_out(cc):
                    plb = plbs[cc]
                    for n in range(4):
                        b = nextbank()
                        P.add("pe", lambda e, n=n, b=b: e.matmul(ps[:, b, :], lhsT=PWt[:, cc, :], rhs=plb[:, n * 512:(n + 1) * 512], start=True, stop=True),
                              r=["PWt", ("plb", cc)], w=[PB(b)])
                        P.add("dve", lambda e, n=n, b=b: e.scalar_tensor_tensor(out=yT[:, 4 + cc, n * 512:(n + 1) * 512], in0=ps[:, b, :], scalar=pv(6 + cc),
                                                                              in1=gTc[:, cc, n * 512:(n + 1) * 512], op0=ALU.mult, op1=ALU.mult),
                              r=[PB(b), "pvec", "gTc"], w=["yT"])

                c_sums(0)
                c_pool(0)
                c_sums(1)
                c_pool(1)
                c_out(0)
                c_out(1)
                P.barrier(scratch)

                uu, uvv, ugd = ub + 9, ub + 10, ub + 11
                AR.reset()
                guT = AR.alloc(2 * S, BF16).rearrange("p (c n) -> p c n", c=2)
                gv = AR.alloc(NT * 256).rearrange("p (t n) -> p t n", t=NT)
                vn = AR.alloc(NT * 256, BF16).rearrange("p (t n) -> p t n", t=NT)
                wsT = AR.alloc(4 * 128, BF16).rearrange("p (h n) -> p h n", h=4)
                bsb = AR.alloc(2 * 128).rearrange("p (c n) -> p c n", c=2)
                bias2 = AR.alloc(2 * 512).rearrange("p (c q n) -> p c q n", c=2, q=4)
                gsm = [AR.alloc(512, BF16) for _ in range(2)]
                s1 = AR.alloc(16)
                s2 = AR.alloc(16)
                mu = AR.alloc(16)
                rsd = AR.alloc(16)
                jk = AR.alloc(256)
                tmx = [AR.alloc(512) for _ in range(2)]
                P.add("pool", lambda e: e.dma_start(out=wsT, in_=swt_d[l]), r=["BAR"], w=["wsT"], dma="wsT")
                P.add("sp", lambda e: e.dma_start(out=bsb, in_=sbb_d[l]), r=["BAR"], w=["bsb"], dma="bsb")
                for cc in range(2):
                    for n in range(4):
                        b = nextbank()
                        proj_fm(uu, cc * 128, n, b)
                        P.add("act", lambda e, b=b, cc=cc, n=n: e.activation(out=guT[:, cc, n * 512:(n + 1) * 512], in_=ps[:, b, :], func=AF.Gelu_apprx_tanh),
                              r=[PB(b)], w=["guT"])
                for t2 in range(8):
                    b = nextbank()
                    for tt in range(2):
                        proj_tm(uvv, 0, 256, 2 * t2 + tt, b, tt * 256)
                    for tt in range(2):
                        t = 2 * t2 + tt
                        P.add("act", lambda e, b=b, t=t, tt=tt: e.activation(out=gv[:, t, :], in_=ps[:, b, tt * 256:(tt + 1) * 256], func=AF.Gelu_apprx_tanh, accum_out=s1[:, t:t + 1]),
                              r=[PB(b)], w=[("gv", t), "s1"])
                    for tt in range(2):
                        t = 2 * t2 + tt
                        P.add("dve", lambda e, t=t: e.scalar_tensor_tensor(out=jk, in0=gv[:, t, :], scalar=1.0, in1=gv[:, t, :], op0=ALU.mult, op1=ALU.mult,
                                                                          accum_out=s2[:, t:t + 1]),
                              r=[("gv", t)], w=["s2", "jk"])
                for u in (uu, uvv):
                    done_unit(u)
                P.add("dve", lambda e: e.tensor_scalar(out=mu, in0=s1, scalar1=1.0 / 256.0, scalar2=None, op0=ALU.mult), r=["s1"], w=["mu"])
                P.add("dve", lambda e: e.tensor_tensor(out=rsd, in0=mu, in1=mu, op=ALU.mult), r=["mu"], w=["rsd"])
                P.add("dve", lambda e: e.scalar_tensor_tensor(out=rsd, in0=s2, scalar=1.0 / 256.0, in1=rsd, op0=ALU.mult, op1=ALU.subtract), r=["s2", "rsd"], w=["rsd"])
                P.add("act", lambda e: e.activation(out=rsd, in_=rsd, func=AF.Sqrt, bias=cst[:, 1:2], scale=1.0), r=["rsd", "cst"], w=["rsd"])
                P.add("dve", lambda e: e.reciprocal(out=rsd, in_=rsd), r=["rsd"], w=["rsd"])
                for t in range(NT):
                    P.add("dve", lambda e, t=t: e.tensor_scalar(out=vn[:, t, :], in0=gv[:, t, :], scalar1=mu[:, t:t + 1], scalar2=rsd[:, t:t + 1], op0=ALU.subtract, op1=ALU.mult),
                          r=[("gv", t), "mu", "rsd"], w=[("vn", t)])
                bw = nextbank()
                for hd in range(4):
                    P.add("pe", lambda e, hd=hd, bw=bw: e.matmul(ps[64 * (hd % 2):64 * (hd % 2) + 64, bw, (hd // 2) * 128:(hd // 2) * 128 + 128], lhsT=ones_bf[:, 0:64], rhs=wsT[:, hd, :],
                                                                 start=True, stop=True, skip_group_check=True, tile_position=(0, 64 * (hd % 2))),
                          r=["ones", "wsT", "BAR"], w=[PB(bw)])
                for cc in range(2):
                    for q in range(4):
                        P.add("dve", lambda e, cc=cc, bw=bw, q=q: e.scalar_tensor_tensor(out=bias2[:, cc, q, :], in0=ps[:, bw, cc * 128:(cc + 1) * 128], scalar=pv(10 + cc), in1=bsb[:, cc, :],
                                                                                        op0=ALU.mult, op1=ALU.add),
                              r=[PB(bw), "pvec", "bsb"], w=["bias2"])
                for cc in range(2):
                    for n in range(4):
                        b = nextbank()
                        for q in range(4):
                            tq = 4 * n + q
                            for h2 in range(2):
                                hd = 2 * cc + h2
                                P.add("pe", lambda e, b=b, q=q, tq=tq, h2=h2, hd=hd: e.matmul(ps[64 * h2:64 * h2 + 64, b, q * 128:(q + 1) * 128], lhsT=vn[:, tq, hd * 64:(hd + 1) * 64], rhs=wsT[:, hd, :],
                                                                                             start=True, stop=True, skip_group_check=True, tile_position=(0, 64 * h2)),
                                      r=[("vn", tq), "wsT"], w=[PB(b)])
                        tm = tmx[n % 2]
                        P.add("dve", lambda e, b=b, cc=cc, tm=tm: e.scalar_tensor_tensor(out=tm, in0=ps[:, b, :], scalar=pv(8 + cc),
                                                                                        in1=bias2[:, cc].rearrange("p q n -> p (q n)"), op0=ALU.mult, op1=ALU.add),
                              r=[PB(b), "pvec", "bias2"], w=[("tm", n % 2)])
                        P.add("dve", lambda e, cc=cc, n=n, tm=tm: e.tensor_tensor(out=tm, in0=tm, in1=guT[:, cc, n * 512:(n + 1) * 512], op=ALU.mult),
                              r=[("tm", n % 2), "guT"], w=[("tm", n % 2)])
                        b2 = nextbank()
                        proj_fm(ugd, cc * 128, n, b2)
                        gs = gsm[n % 2]
                        P.add("act", lambda e, b2=b2, gs=gs: e.activation(out=gs, in_=ps[:, b2, :], func=AF.Silu),
                              r=[PB(b2)], w=[("gsm", n % 2)])
                        P.add("pool", lambda e, cc=cc, n=n, tm=tm, gs=gs: e.tensor_tensor(out=yT[:, 6 + cc, n * 512:(n + 1) * 512], in0=tm, in1=gs, op=ALU.mult),
                              r=[("tm", n % 2), ("gsm", n % 2)], w=["yT"])
                done_unit(ugd)
                if dbg and li == 0:
                    P.add("sp", lambda e: e.dma_start(out=dbg_y, in_=yT[:]), r=["yT"], dma="dbg")
                P.barrier(scratch)

                uo = ub + 12
                AR.reset()
                last = (li == depth - 1)
                if not last:
                    nrm = norm_setup(normg_d[l + 1:l + 2, :])
                elif final_norm:
                    nrm = norm_setup(fg_d)
                for t in range(NT):
                    for half in range(2):
                        b = 4 + (2 * t + half) % 4
                        slot = (uo + 2 * half) % NSLOT
                        wv = wo_view(slot)
                        for yc in range(8):
                            P.add("pe", lambda e, b=b, wv=wv, yc=yc: e.matmul(ps[:, b, :], lhsT=yT[:, yc, t * 128:(t + 1) * 128], rhs=wv[:, yc, :],
                                                                             start=(yc == 0), stop=(yc == 7)),
                                  r=["yT", ("W", slot), ("W", slot + 1)], w=[PB(b)])
                        P.add("dve", lambda e, b=b, half=half: e.tensor_tensor(out=xres[:, t, half * 512:(half + 1) * 512], in0=ps[:, b, :], in1=xres[:, t, half * 512:(half + 1) * 512], op=ALU.add),
                              r=[PB(b), ("x", t)], w=[("x", t)])
                    if not last:
                        norm_stats_act(nrm, t)
                        if t >= 1:
                            norm_stats_dve(nrm, t - 1)
                            normA_h(nrm, t - 1)
                        if t % 4 == 1 and t >= 5:
                            for t2 in range(t - 5, t - 1):
                                normA_T(nrm, t2)
                    elif final_norm:
                        norm_stats(nrm, t)
                        if t >= 1:
                            final_tile(nrm, t - 1)
                if last and final_norm:
                    final_tile(nrm, NT - 1)
                if not last:
                    norm_stats_dve(nrm, NT - 1)
                    normA_h(nrm, NT - 1)
                    for t2 in range(NT - 4, NT):
                        normA_T(nrm, t2)
                for q in range(4):
                    done_unit(uo + q)

            if not final_norm:
                for t in range(NT):
                    P.add("sp", lambda e, t=t: e.dma_start(out=out_d[t * 128:(t + 1) * 128, :], in_=xres[:, t, :]), r=[("x", t)], dma=f"o{t % 2}")
        _run()
    return nc


def _na_tables(rpb):
    p = np.arange(128)
    hlf, kc = p // 64, p % 64
    tab = np.zeros((DEPTH, 8, 128, TABW), np.float32)
    mask = np.full((128, TABW), NEG, np.float32)

    def fill(col0, nrows, kr_of_half, r_of_ri, need_edge_rule):
        q = np.arange(nrows * 64)
        ri, qc = q // 64, q % 64
        cs = np.clip(qc - 8, 0, 48)
        col_ok = (kc[:, None] >= cs[None, :]) & (kc[:, None] < cs[None, :] + 16)
        dc = np.clip(kc[:, None] - qc[None, :], -15, 15) + 15
        kr = kr_of_half(hlf)[:, None]
        r = r_of_ri(ri)[None, :]
        valid = col_ok & np.ones_like(kr + r, bool)
        if need_edge_rule:
            rs = np.clip(r - 4, 0, 24)
            inwin = (r >= kr - 3) & (r <= kr + 4)
            valid = valid & (kr >= rs) & (kr <= rs + 7) & (~inwin)
        drr = np.clip(kr - r, -7, 7) + 7
        g = rpb[:, :, drr, dc]
        tab[:, :, :, col0:col0 + nrows * 64] = np.where(valid[None, None], g, 0.0)
        mask[:, col0:col0 + nrows * 64] = np.where(valid, 0.0, NEG)

    q = np.arange(512)
    ri, qc = q // 64, q % 64
    cs = np.clip(qc - 8, 0, 48)
    col_ok = (kc[:, None] >= cs[None, :]) & (kc[:, None] < cs[None, :] + 16)
    dc = np.clip(kc[:, None] - qc[None, :], -15, 15) + 15
    drr = np.broadcast_to((3 - ri)[None, :] + 7, (128, 512))
    g = rpb[:, :, drr, dc]
    tab[:, :, :, 0:512] = np.where(col_ok[None, None], g, 0.0)
    mask[:, 0:512] = np.where(col_ok, 0.0, NEG)
    for j, col0 in ((2, 512), (3, 768), (12, 1024), (13, 1280)):
        base_r = 0 if j < 8 else 28
        fill(col0, 4, lambda h, j=j: 2 * j + h, lambda ri, base_r=base_r: base_r + ri, True)
    return tab, mask


def _pvec(conv_dw_w, conv_dw_b, conv_ln_g, conv_ln_b, pool_scale, sgu_ln_g, sgu_ln_b):
    pv = np.zeros((128, DEPTH * NVL + 34), np.float32)
    pv[:, DEPTH * NVL + 32] = ((np.arange(128) // 32) % 2 == 0)
    pv[:, DEPTH * NVL + 33] = ((np.arange(128) // 32) % 2 == 1)
    col = lambda v: v.reshape(2, 128).T
    for l in range(DEPTH):
        b = l * NVL
        pv[:, b + 0:b + 2] = col(conv_dw_b[l])
        pv[:, b + 2:b + 4] = col(conv_ln_g[l])
        pv[:, b + 4:b + 6] = col(conv_ln_b[l])
        pv[:, b + 6:b + 8] = col(pool_scale[l])
        pv[:, b + 8:b + 10] = col(sgu_ln_g[l])
        pv[:, b + 10:b + 12] = col(sgu_ln_b[l])
        w = conv_dw_w[l]
        wz = np.concatenate([w, np.zeros((1, 256), np.float32)], 0)
        for cc in range(2):
            for a in range(2):
                ch = cc * 128 + 64 * a + np.arange(64)
                for j in range(16):
                    v = np.zeros(128, np.float32)
                    first, second = wz[2 * j, ch], wz[2 * j + 1, ch]
                    if a == 0:
                        v[0:64], v[64:128] = first, second
                    else:
                        v[64:128], v[0:64] = first, second
                    pv[:, b + 12 + (cc * 2 + a) * 16 + j] = v
    for cc in range(2):
        for p in range(128):
            w = POOL_W[2 * cc + p // 64]
            for i, t in enumerate(list(range(8)) + list(range(S - 8, S))):
                lo = min(max(t - w // 2, 0), S)
                hi = min(max(t - w // 2 + w, 0), S)
                pv[p, DEPTH * NVL + cc * 16 + i] = w / (hi - lo)
    return pv


_NC_CACHE = {}


def _get_nc(key, **kw):
    if key not in _NC_CACHE:
        _NC_CACHE[key] = build(**kw)
    return _NC_CACHE[key]


def _host_inputs(x, norm_g, w_in, na_rpb, conv_dw_w, conv_dw_b, conv_ln_g, conv_ln_b, conv_pw,
                 pool_w, pool_scale, sgu_ln_g, sgu_ln_b, sgu_w, sgu_b, w_out, final_g):
    f = lambda a: np.ascontiguousarray(np.asarray(a, dtype=np.float32))
    tab, mask = _na_tables(f(na_rpb))
    sgu_wT = np.ascontiguousarray(f(sgu_w).transpose(0, 3, 1, 2))
    sb = f(sgu_b)
    sgu_bb = np.ascontiguousarray(np.repeat(sb.reshape(DEPTH, 2, 2, 1, 128), 64, axis=3)
                                  .reshape(DEPTH, 2, 128, 128).transpose(0, 2, 1, 3))
    shared = {
        "norm_g": f(norm_g), "w_in": f(w_in), "w_out": f(w_out), "final_g": f(final_g).reshape(1, D),
        "na_tab": tab, "na_mask": mask,
        "pvec": _pvec(f(conv_dw_w), f(conv_dw_b), f(conv_ln_g), f(conv_ln_b), f(pool_scale), f(sgu_ln_g), f(sgu_ln_b)),
        "conv_pw": f(conv_pw), "pool_w": f(pool_w), "sgu_wT": sgu_wT, "sgu_bb": sgu_bb,
        "ident": np.eye(128, dtype=np.float32).astype(ml_dtypes.bfloat16),
    }
    return shared


def kernel(**inputs):
    x = np.ascontiguousarray(np.asarray(inputs["x"], dtype=np.float32))
    shared = _host_inputs(**inputs)
    nc = _get_nc("full", depth=DEPTH, first_layer=0, final_norm=True)
    in_maps = [dict(shared, x=x[b]) for b in range(NCORES)]
    res = run_bass_kernel_spmd(nc, in_maps, core_ids=list(range(NCORES)))
    return np.stack([np.asarray(r["out"], dtype=np.float32) for r in res.results], axis=0)
```

```python
import contextlib
import numpy as np
import ml_dtypes
import concourse.bass as bass
import concourse.mybir as mybir
from concourse.bass_utils import run_bass_kernel_spmd

F32 = mybir.dt.float32
BF16 = mybir.dt.bfloat16
AF = mybir.ActivationFunctionType
ALU = mybir.AluOpType
AX = mybir.AxisListType

S, D, DEPTH, NCORES = 2048, 1024, 2, 8
NT = 16
DIN = 3072
NVL = 76
NEG = -30000.0
NBLK = 11
TABW = 512 + 4 * 256
SPECIAL = {2: [(0, 5)], 3: [(0, 6), (1, 7)], 12: [(14, 8), (15, 9)], 13: [(15, 10)]}
BLOCKS = [(8, 6), (8, 7), (8, 8), (8, 9), (8, 10), (2, 0), (3, 0), (3, 1), (12, 14), (12, 15), (13, 15)]
POOL_W = (2, 4, 8, 16)


class Prog:
    def __init__(self, nc, plan=None):
        self.nc = nc
        self.plan = plan
        self.engs = {"pe": nc.tensor, "act": nc.scalar, "dve": nc.vector,
                     "pool": nc.gpsimd, "sp": nc.sync}
        self.n = 0
        self.info = []
        self.cnt = []
        self.needed = []
        self.lastw = {}
        self.readers = {}
        self.last_on = {}
        self.bar = None
        self.sems = {}
        self.counts = {}
        self.waited = {}

    def sem(self, key):
        if key not in self.sems:
            self.sems[key] = self.nc.alloc_semaphore("s_" + "_".join(str(k) for k in key))
        return self.sems[key]

    def add(self, eng, fn, r=(), w=(), dma=None, extra=(), force=()):
        idx = self.n
        self.n += 1
        hard = set(extra) | set(force)
        soft = set()
        if self.bar is not None:
            hard.add(self.bar)
        for t in r:
            lw = self.lastw.get(t)
            if lw is not None:
                hard.add(lw)
        for t in w:
            lw = self.lastw.get(t)
            if lw is not None:
                hard.add(lw)
            for rd in self.readers.get(t, ()):
                soft.add(rd)
        for t in w:
            self.lastw[t] = idx
            self.readers[t] = []
        for t in r:
            self.readers.setdefault(t, []).append(idx)
        keep = []
        for d in hard | soft:
            deng, ddma = self.info[d]
            if ddma is None and dma is None and deng == eng:
                if eng == "pe" and d not in force:
                    continue
            keep.append(d)
        self.info.append((eng, dma))
        self.needed.append(dma is not None)
        if dma is None:
            self.last_on[eng] = idx
        if self.plan is None:
            for d in keep:
                self.needed[d] = True
            self.cnt.append(None)
            return idx
        E = self.engs[eng]
        need = {}
        for d in keep:
            deng, ddma = self.info[d]
            key = ("d", ddma) if ddma is not None else ("e", deng)
            need[key] = max(need.get(key, 0), self.cnt[d])
        for key, val in need.items():
            wk = (eng, key)
            if self.waited.get(wk, 0) >= val:
                continue
            self.waited[wk] = val
            E.wait_ge(self.sem(key), val)
        ins = fn(E)
        if dma is not None:
            key = ("d", dma)
            self.counts[key] = self.counts.get(key, 0) + 16
            self.cnt.append(self.counts[key])
            ins.then_inc(self.sem(key), 16)
        elif self.plan.needed[idx]:
            key = ("e", eng)
            self.counts[key] = self.counts.get(key, 0) + 1
            self.cnt.append(self.counts[key])
            ins.then_inc(self.sem(key), 1)
        else:
            self.cnt.append(None)
        return idx

    def barrier(self, scratch):
        lasts = list(self.last_on.values())
        self.bar = None
        self.bar = self.add("dve", lambda e: e.memset(scratch, 0.0), extra=lasts, w=["BAR"])

    def finish(self, eng="sp"):
        E = self.engs[eng]
        for key, val in self.counts.items():
            if self.waited.get((eng, key), 0) < val:
                E.wait_ge(self.sem(key), val)


class Arena:
    def __init__(self, ap_f32, nwords):
        self.base, self.n, self.off = ap_f32, nwords, 0

    def alloc(self, cols, dt=F32):
        words = cols if dt == F32 else (cols + 1) // 2
        assert self.off + words <= self.n, ("arena overflow", self.off, words, self.n)
        v = self.base[:, self.off:self.off + words]
        self.off += words
        return v if dt == F32 else v.bitcast(BF16)

    def reset(self, off=0):
        self.off = off


def blocks(j):
    lo, hi = max(0, j - 2), min(15, j + 2)
    sp = SPECIAL.get(j, [])
    spm = {m for m, _ in sp}
    while lo in spm:
        lo += 1
    while hi in spm:
        hi -= 1
    return lo, hi, sp


def complete_after(m):
    return 3 if m == 0 else min(15, m + 2)


def build(depth=DEPTH, first_layer=0, final_norm=True, dbg=False):
    nc = bass.Bass("TRN2", target_bir_lowering=False)
    dr = lambda name, shape, dt=F32: nc.dram_tensor(name, shape, dt, kind="ExternalInput").ap()
    x_d = dr("x", [S, D])
    normg_d = dr("norm_g", [DEPTH, D])
    win_d = dr("w_in", [DEPTH, D, DIN])
    wout_d = dr("w_out", [DEPTH, D, D])
    fg_d = dr("final_g", [1, D])
    tab_d = dr("na_tab", [DEPTH, 8, 128, TABW])
    mask_d = dr("na_mask", [128, TABW])
    pvec_d = dr("pvec", [128, DEPTH * NVL + 34])
    pw_d = dr("conv_pw", [DEPTH, 256, 256])
    poolw_d = dr("pool_w", [DEPTH, 4, 64, 64])
    swt_d = dr("sgu_wT", [DEPTH, 128, 4, 128])
    sbb_d = dr("sgu_bb", [DEPTH, 128, 2, 128])
    ident_d = dr("ident", [128, 128], BF16)
    out_d = nc.dram_tensor("out", [S, D], F32, kind="ExternalOutput").ap()
    if dbg:
        dbg_y = nc.dram_tensor("dbg_y", [128, 8, S], BF16, kind="ExternalOutput").ap()
        dbg_h = nc.dram_tensor("dbg_h", [128, 8, S], BF16, kind="ExternalOutput").ap()

    st = contextlib.ExitStack()
    with st:
        sb = lambda name, shape, dt: st.enter_context(nc.sbuf_tensor(name, shape, dt))
        xres = sb("xres", [128, NT, D], F32)
        hT = sb("hT", [128, 8, S], BF16)
        yT = sb("yT", [128, 8, S], BF16)
        NSLOT = 6
        Wr = sb("Wr", [128, NSLOT, 8, 256], BF16)
        ident = sb("ident_sb", [128, 128], BF16)
        ones_bf = sb("ones_bf", [128, 128], BF16)
        ident32 = sb("ident32", [128, 128], F32)
        i2f = sb("i2f", [128, 64], F32)
        i2b = sb("i2b", [128, 64], BF16)
        onesm = sb("onesm", [128, 128], BF16)
        pvec = sb("pvec_sb", [128, DEPTH * NVL + 34], F32)
        maskb = sb("maskb", [128, TABW], BF16)
        cst = sb("cst", [128, 16], F32)
        ARW = 12424
        arena_t = sb("arena", [128, ARW], F32)
        ps = st.enter_context(nc.psum_tensor("ps", [128, 8, 512], F32))
        psflat = ps[:].rearrange("p b n -> p (b n)")
        AR = Arena(arena_t[:], ARW)

        scratch = cst[:, 2:3]
        PB = lambda b: ("ps", b)

        def body(P):
            _body(P)

        def _run():
            P1 = Prog(nc, plan=None)
            body(P1)
            P2 = Prog(nc, plan=P1)
            body(P2)
            assert P1.n == P2.n
            P2.finish()

        def _body(P):
            def norm_setup(g_row):
                d = {"ss": AR.alloc(16), "rstd": AR.alloc(16), "junk": AR.alloc(1024),
                     "hts": [AR.alloc(1024, BF16) for _ in range(8)],
                     "ob": [AR.alloc(1024) for _ in range(2)], "gbc": AR.alloc(D)}
                P.add("sp", lambda e: e.dma_start(out=d["gbc"], in_=g_row.partition_broadcast(128)), w=["gbc"], dma="g")
                return d

            def norm_stats_act(d, t):
                ss, rstd = d["ss"], d["rstd"]
                P.add("act", lambda e: e.activation(out=d["junk"], in_=xres[:, t, :], func=AF.Square, accum_out=ss[:, t:t + 1]),
                      r=[("x", t)], w=[("ss", t), "junk"])
                P.add("act", lambda e: e.activation(out=rstd[:, t:t + 1], in_=ss[:, t:t + 1], func=AF.Sqrt, bias=cst[:, 0:1], scale=1.0 / D),
                      r=[("ss", t), "cst"], w=[("rstd", t)])

            def norm_stats_dve(d, t):
                rstd = d["rstd"]
                P.add("dve", lambda e: e.reciprocal(out=rstd[:, t:t + 1], in_=rstd[:, t:t + 1]), r=[("rstd", t)], w=[("rstd", t)])

            def norm_stats(d, t):
                norm_stats_act(d, t)
                norm_stats_dve(d, t)

            def normA_h(d, t):
                hb = d["hts"][t % 8]
                P.add("dve", lambda e: e.scalar_tensor_tensor(out=hb, in0=xres[:, t, :], scalar=d["rstd"][:, t:t + 1], in1=d["gbc"],
                                                              op0=ALU.mult, op1=ALU.mult),
                      r=[("x", t), ("rstd", t), "gbc"], w=[("ht", t % 8)])

            def normA_tile(d, t, after_stt=None):
                normA_h(d, t)
                if after_stt is not None:
                    after_stt()
                normA_T(d, t)

            def normA_T(d, t):
                hb = d["hts"][t % 8]
                bk = t % 4
                psb = ps[:, bk, :].bitcast(BF16)
                for c in range(8):
                    P.add("pe", lambda e, c=c: e.transpose(out=psb[:, c * 128:(c + 1) * 128], in_=hb[:, c * 128:(c + 1) * 128], identity=ident[:]),
                          r=[("ht", t % 8), "ident"], w=[PB(bk)])
                P.add("act", lambda e: e.activation(out=hT[:, :, t * 128:(t + 1) * 128],
                                                    in_=psb.rearrange("p (c n) -> p c n", c=8), func=AF.Copy),
                      r=[PB(bk)], w=["hT"])

            def final_tile(d, t):
                o = d["ob"][t % 2]
                P.add("dve", lambda e: e.scalar_tensor_tensor(out=o, in0=xres[:, t, :], scalar=d["rstd"][:, t:t + 1], in1=d["gbc"], op0=ALU.mult, op1=ALU.mult),
                      r=[("x", t), ("rstd", t), "gbc"], w=[("ob", t % 2)])
                P.add("sp", lambda e: e.dma_start(out=out_d[t * 128:(t + 1) * 128, :], in_=o), r=[("ob", t % 2)], dma=f"o{t % 2}")

            P.add("sp", lambda e: e.dma_start(out=ident[:], in_=ident_d), w=["ident"], dma="c0")
            P.add("sp", lambda e: e.dma_start(out=pvec[:], in_=pvec_d), w=["pvec"], dma="c1")
            AR.reset()
            nrm_first = norm_setup(normg_d[first_layer:first_layer + 1, :])
            xload = []
            for t in range(NT):
                xload.append(P.add("sp", lambda e, t=t: e.dma_start(out=xres[:, t, :], in_=x_d[t * 128:(t + 1) * 128, :]),
                                   w=[("x", t)], dma=f"x{t}"))
            P.add("pool", lambda e: e.dma_start(out=maskb[:], in_=mask_d), w=["maskb"], dma="c2")
            P.add("dve", lambda e: e.memset(ones_bf[:], 1.0), w=["ones"])
            P.add("dve", lambda e: e.tensor_copy(out=ident32[:], in_=ident[:]), r=["ident"], w=["ident32"])
            P.add("dve", lambda e: e.tensor_tensor(out=i2f[:], in0=ident32[:, 0:64], in1=ident32[:, 64:128], op=ALU.add), r=["ident32"], w=["i2"])
            P.add("dve", lambda e: e.tensor_copy(out=i2b[:], in_=i2f[:]), r=["i2"], w=["i2"])
            P.add("dve", lambda e: e.memset(onesm[:], 1.0 / 256.0), w=["onesm"])
            P.add("dve", lambda e: e.memset(cst[:, 0:1], 1e-6), w=["cst"])
            P.add("dve", lambda e: e.memset(cst[:, 1:2], 1e-5), w=["cst"])

            units = []
            for l in range(first_layer, first_layer + depth):
                for c0 in (0, 256, 512, 2048,
                           768, 1024, 2304,
                           1280, 2560,
                           1536, 1792, 2816):
                    units.append(("in", l, c0))
                for c0 in (0, 256, 512, 768):
                    units.append(("out", l, c0))
            UPL = 16
            issued = [0]

            def load_unit(i):
                kind, l, c0 = units[i]
                slot = i % NSLOT
                if kind == "in":
                    src = win_d[l, :, c0:c0 + 256].rearrange("(c p) n -> p c n", p=128)
                    P.add("pool", lambda e: e.dma_start(out=Wr[:, slot], in_=src), w=[("W", slot)], dma=f"W{slot}",
                          extra=([xload[5]] if i < 2 else []))
                elif c0 % 512 == 256:
                    s0 = slot - 1
                    assert s0 % 2 == 0 and s0 >= 0
                    src = wout_d[l, :, c0 - 256:c0 + 256].rearrange("(c p) n -> p c n", p=128)
                    P.add("pool", lambda e: e.dma_start(out=wo_view(s0), in_=src), w=[("W", s0), ("W", s0 + 1)], dma=f"W{s0}")

            def wo_view(slot):
                return Wr[:, slot:slot + 2].rearrange("p s c n -> p (s c n)").rearrange("p (c m) -> p c m", c=8)

            def done_unit(i):
                nxt = i + NSLOT
                while issued[0] <= nxt and issued[0] < len(units):
                    load_unit(issued[0])
                    issued[0] += 1

            for i in range(4):
                load_unit(i)
            issued[0] = 4

            bank_rr = [0]

            def nextbank():
                b = bank_rr[0]
                bank_rr[0] = (b + 1) % 8
                return b

            def proj_fm(ui, col, n, bank):
                slot = ui % NSLOT
                for c in range(8):
                    P.add("pe", lambda e, c=c: e.matmul(ps[:, bank, :], lhsT=Wr[:, slot, c, col:col + 128],
                                                        rhs=hT[:, c, n * 512:(n + 1) * 512], start=(c == 0), stop=(c == 7)),
                          r=[("W", slot), "hT", "BAR"], w=[PB(bank)])

            def proj_tm(ui, col, ncol, t, bank, off):
                slot = ui % NSLOT
                for c in range(8):
                    P.add("pe", lambda e, c=c: e.matmul(ps[:, bank, off:off + ncol], lhsT=hT[:, c, t * 128:(t + 1) * 128],
                                                        rhs=Wr[:, slot, c, col:col + ncol], start=(c == 0), stop=(c == 7)),
                          r=[("W", slot), "hT", "BAR"], w=[PB(bank)])

            for li in range(depth):
                l = first_layer + li
                ub = li * UPL
                pv = lambda k: pvec[:, l * NVL + k:l * NVL + k + 1]

                if li == 0:
                    nrm = nrm_first
                    norm_stats(nrm, 0)
                    norm_stats(nrm, 1)
                    for t in range(NT):
                        if t + 2 < NT:
                            norm_stats_act(nrm, t + 2)
                            normA_tile(nrm, t, after_stt=lambda t=t: norm_stats_dve(nrm, t + 2))
                        else:
                            normA_tile(nrm, t)
                if dbg and li == 0:
                    P.add("sp", lambda e: e.dma_start(out=dbg_h, in_=hT[:]), r=["hT"], dma="dbg")
                if li == 0:
                    while issued[0] < min(NSLOT, len(units)):
                        load_unit(issued[0])
                        issued[0] += 1
                P.barrier(scratch)

                uq, uk, uv, ug = ub + 0, ub + 1, ub + 2, ub + 3
                for g in range(2):
                    AR.reset()
                    qT = AR.alloc(S, BF16)
                    kTm = [AR.alloc(S, BF16) for _ in range(2)]
                    gT = AR.alloc(S, BF16)
                    vtok = AR.alloc(NT * 128, BF16).rearrange("p (t n) -> p t n", t=NT)
                    E = AR.alloc(4 * TABW, BF16).rearrange("p (h n) -> p h n", h=4)
                    NPT = 8
                    PT = [AR.alloc(512, BF16) for _ in range(NPT)]
                    tyrc = [AR.alloc(256).rearrange("p (a n) -> p a n", a=2) for _ in range(4)]
                    tyb = [t_[:, 0, :] for t_ in tyrc]
                    rcb = [t_[:, 1, :] for t_ in tyrc]
                    P.add("pool", lambda e, g=g: e.dma_start(out=E, in_=tab_d[l, 4 * g:4 * g + 4].rearrange("h p n -> p h n")),
                          r=["BAR"], w=["E"] + [("E", hh_) for hh_ in range(4)], dma="E")
                    def e_prep(hh):
                        P.add("dve", lambda e: e.tensor_tensor(out=E[:, hh, :], in0=E[:, hh, :], in1=maskb[:], op=ALU.add),
                              r=["E", "maskb"], w=[("E", hh)])
                        P.add("act", lambda e: e.activation(out=E[:, hh, :], in_=E[:, hh, :], func=AF.Exp),
                              r=[("E", hh)], w=[("E", hh)])

                    for n in range(4):
                        if n >= 2:
                            e_prep(2 * (n - 2))
                            e_prep(2 * (n - 2) + 1)
                        b = nextbank()
                        proj_fm(uq, g * 128, n, b)
                        P.add("dve", lambda e, n=n, b=b: e.tensor_scalar(out=qT[:, n * 512:(n + 1) * 512], in0=ps[:, b, :], scalar1=32.0 ** -0.5, scalar2=None, op0=ALU.mult),
                              r=[PB(b)], w=["qT"])
                        b = nextbank()
                        proj_fm(uk, g * 128, n, b)
                        for par in range(2):
                            mcol = pvec[:, DEPTH * NVL + 32 + par:DEPTH * NVL + 33 + par]
                            P.add("act", lambda e, n=n, b=b, par=par, mcol=mcol: e.activation(out=kTm[par][:, n * 512:(n + 1) * 512], in_=ps[:, b, :], func=AF.Identity, scale=mcol),
                                  r=[PB(b), "pvec"], w=["kT"])
                        b = nextbank()
                        proj_fm(ug, g * 128, n, b)
                        P.add("act", lambda e, n=n, b=b: e.activation(out=gT[:, n * 512:(n + 1) * 512], in_=ps[:, b, :], func=AF.Silu),
                              r=[PB(b)], w=["gT"])
                    for t4 in range(4):
                        b = nextbank()
                        for tt in range(4):
                            proj_tm(uv, g * 128, 128, 4 * t4 + tt, b, tt * 128)
                        P.add("act", lambda e, t4=t4, b=b: e.activation(out=vtok[:, 4 * t4:4 * t4 + 4, :], in_=ps[:, b, :].rearrange("p (t n) -> p t n", t=4), func=AF.Copy),
                              r=[PB(b)], w=["vtok"])
                    if g == 1:
                        for u in (uq, uk, uv, ug):
                            done_unit(u)
                    for b in range(4, 8):
                        P.add("dve", lambda e, b=b: e.memset(ps[:, b, :], 0.0), w=[PB(b)])
                    accy = psflat[:, 4 * 512:6 * 512]
                    accs = psflat[:, 6 * 512:8 * 512]
                    acc2 = psflat[:, 4 * 512:8 * 512].rearrange("p (a n) -> p a n", a=2)
                    ACCY = lambda m: PB(4 + (m % 8) // 4)
                    ACCS = lambda m: PB(6 + (m % 8) // 4)
                    MULENG = {0: "dve", 1: "dve", 2: "dve", 3: "dve"}
                    seq = []
                    for j in range(16):
                        seq.append(("reg", j))
                        if j == 3:
                            seq.append(("xt", 0))
                        if j == 13:
                            seq.append(("xb", 0))
                    pos_of = {kt: i for i, kt in enumerate(seq)}

                    def parts_of(kind, j):
                        out = []
                        if kind == "reg":
                            for h in range(2):
                                kr = 2 * j + h
                                lo_, hi_ = max(0, kr - 3), min(31, kr + 4)
                                out.append((j, h, lo_, hi_, (lo_ - (kr - 3)) * 64))
                            return out, 0
                        jks, rows, t0 = ((2, 3), (0, 3), 512) if kind == "xt" else ((12, 13), (28, 31), 1024)
                        for k, jk in enumerate(jks):
                            for h in range(2):
                                out.append((jk, h, rows[0], rows[1], 256 * k))
                        return out, t0

                    last_pos = {}
                    for m in range(16):
                        cand = [pos_of[("reg", j)] for j in range(16) if max(0, 2 * j - 3) <= 2 * m + 1 and min(31, 2 * j + 5) >= 2 * m]
                        if m <= 1:
                            cand.append(pos_of[("xt", 0)])
                        if m >= 14:
                            cand.append(pos_of[("xb", 0)])
                        last_pos[m] = max(cand)

                    def s_burst(i):
                        kind, j = seq[i]
                        pts, t0 = parts_of(kind, j)
                        c_lo = min(c0 for (_, _, _, _, c0) in pts)
                        c_hi = max(c0 + (hi_ - lo_ + 1) * 64 for (_, _, lo_, hi_, c0) in pts)
                        for par in range(2):
                            for (jk, h, lo_, hi_, c0) in pts:
                                n_ = (hi_ - lo_ + 1) * 64
                                for sb_ in range(2):
                                    for g2 in range(2):
                                        hh = 2 * g2 + par
                                        rs_ = slice(64 * g2, 64 * g2 + 64)
                                        k0 = jk * 128 + 64 * h + 32 * sb_
                                        o0 = 64 * h + 32 * sb_
                                        P.add("pe", lambda e, hh=hh, rs_=rs_, k0=k0, o0=o0, lo_=lo_, c0=c0, n_=n_, par=par, g2=g2: e.matmul(
                                            ps[o0:o0 + 32, hh, c0:c0 + n_], lhsT=kTm[par][rs_, k0:k0 + 32], rhs=qT[rs_, lo_ * 64:lo_ * 64 + n_],
                                            start=True, stop=True, skip_group_check=True, tile_position=(64 * g2, o0)),
                                            r=["qT", "kT"], w=[PB(hh)])
                        for hh in range(4):
                            pt = PT[(4 * i + hh) % NPT]
                            ptk = ("PT", (4 * i + hh) % NPT)
                            P.add("act", lambda e, hh=hh, pt=pt: e.activation(out=pt[:, c_lo:c_hi], in_=ps[:, hh, c_lo:c_hi], func=AF.Exp),
                                  r=[PB(hh)], w=[ptk])
                            P.add(MULENG[hh], lambda e, hh=hh, pt=pt: e.tensor_tensor(out=pt[:, c_lo:c_hi], in0=pt[:, c_lo:c_hi], in1=E[:, hh, t0 + c_lo:t0 + c_hi], op=ALU.mult),
                                  r=[ptk, ("E", hh)], w=[ptk])

                    pvlast = [None]

                    def pv_block(i):
                        kind, j = seq[i]
                        pts, _t0 = parts_of(kind, j)
                        items = {0: {0: [], 1: []}, 1: {0: [], 1: []}}
                        for (jk, h, lo_, hi_, c0) in pts:
                            r0 = lo_
                            while r0 <= hi_:
                                r1 = min(hi_, (r0 // 8) * 8 + 7)
                                for which in range(2):
                                    items[h][which].append((jk, lo_, c0, r0, r1))
                                r0 = r1 + 1

                        def group(h, which, jk, lo_, c0, r0, r1, frc):
                            wdt = (r1 - r0 + 1) * 64
                            oc = (r0 % 16) * 64
                            pc = c0 + (r0 - lo_) * 64
                            bank = (4 if which == 0 else 6) + (r0 % 16) // 8
                            for hh in range(4):
                                pt = PT[(4 * i + hh) % NPT]
                                hs = slice(32 * hh, 32 * hh + 32)
                                acc = accy if which == 0 else accs
                                lhs = vtok[64 * h:64 * h + 64, jk, hs] if which == 0 else ones_bf[64 * h:64 * h + 64, 0:32]
                                pvlast[0] = P.add("pe", lambda e, acc=acc, lhs=lhs, pt=pt, hs=hs, hh=hh: e.matmul(
                                    acc[hs, oc:oc + wdt], lhsT=lhs, rhs=pt[64 * h:64 * h + 64, pc:pc + wdt],
                                    start=False, stop=True, skip_group_check=True, tile_position=(64 * h, 32 * hh)),
                                    r=[("PT", (4 * i + hh) % NPT), "vtok", "ones"], w=[PB(bank)], force=(frc if hh == 0 else []))
                                frc = []

                        for phase in range(2):
                            l0 = items[0][phase]
                            l1 = items[1][1 - phase]
                            frc = [pvlast[0]] if (phase == 1 and pvlast[0] is not None) else []
                            for k in range(max(len(l0), len(l1))):
                                if k < len(l0):
                                    group(0, phase, *l0[k], frc)
                                    frc = []
                                if k < len(l1):
                                    group(1, 1 - phase, *l1[k], frc)
                                    frc = []
                        for m in range(16):
                            if last_pos[m] != i:
                                continue
                            oc = (m % 8) * 128
                            rc, ty = rcb[m % 4], tyb[m % 4]
                            P.add("dve", lambda e, m=m, oc=oc: e.tensor_copy(out=tyrc[m % 4], in_=acc2[:, :, oc:oc + 128]),
                                  r=[ACCY(m), ACCS(m)], w=[("ty", m % 4), ("rc", m % 4)])
                            P.add("dve", lambda e, oc=oc: e.memset(acc2[:, :, oc:oc + 128], 0.0),
                                  r=[ACCY(m), ACCS(m)], w=[ACCY(m), ACCS(m)])
                            def post(rc=rc, ty=ty, m=m):
                                P.add("act", lambda e: e.activation(out=rc, in_=rc, func=AF.Ln), r=[("rc", m % 4)], w=[("rc", m % 4)])
                                P.add("act", lambda e: e.activation(out=rc, in_=rc, func=AF.Exp, scale=-1.0), r=[("rc", m % 4)], w=[("rc", m % 4)])
                                P.add("dve", lambda e: e.tensor_tensor(out=ty, in0=ty, in1=rc, op=ALU.mult),
                                      r=[("ty", m % 4), ("rc", m % 4)], w=[("ty", m % 4)])
                                P.add("dve", lambda e: e.tensor_tensor(out=yT[:, g, m * 128:(m + 1) * 128], in0=ty, in1=gT[:, m * 128:(m + 1) * 128], op=ALU.mult),
                                      r=[("ty", m % 4), "gT"], w=["yT"])
                            posts.append(post)

                    posts = []
                    for i in range(len(seq)):
                        s_burst(i)
                        for p_ in posts:
                            p_()
                        del posts[:]
                        if i > 0:
                            pv_block(i - 1)
                    pv_block(len(seq) - 1)
                    for p_ in posts:
                        p_()
                    del posts[:]
                    if g == 1:
                        P.barrier(scratch)

                ua, ubb, ugb = ub + 4, ub + 5, ub + 6
                AR.reset()
                GLW = S + 30
                gl = AR.alloc(2 * GLW, BF16).rearrange("p (c n) -> p c n", c=2)
                gl0 = AR.alloc(2 * GLW, BF16).rearrange("p (c n) -> p c n", c=2)
                dg = AR.alloc(64 * 64, BF16).rearrange("p (k n) -> p k n", k=64)
                PWs = AR.alloc(2 * 256, BF16).rearrange("p (c n) -> p c n", c=2)
                sig = [AR.alloc(512) for _ in range(2)]
                c32s = [AR.alloc(2 * 512).rearrange("p (c n) -> p c n", c=2) for _ in range(2)]
                cbf = AR.alloc(2 * 512, BF16).rearrange("p (c n) -> p c n", c=2)
                sqb = AR.alloc(2 * 512, BF16).rearrange("p (c n) -> p c n", c=2)
                mean = AR.alloc(512)
                msq = AR.alloc(512)
                rs = msq
                hsb = AR.alloc(2 * 512, BF16).rearrange("p (c n) -> p c n", c=2)
                P.add("pool", lambda e: e.dma_start(out=PWs, in_=pw_d[l].rearrange("(c p) n -> p c n", p=128)), r=["BAR"], w=["PWs"], dma="pw")
                P.add("pool", lambda e: e.memset(gl[:, :, 0:15], 0.0), r=["BAR"], w=["glpad"])
                P.add("pool", lambda e: e.memset(gl[:, :, 15 + S:GLW], 0.0), r=["BAR"], w=["glpad"])
                P.add("pool", lambda e: e.memset(gl0[:, :, GLW - 1:GLW], 0.0), r=["BAR"], w=["gl0pad", "gl0"])
                dgi = [0]

                def diag_some(k):
                    for _ in range(k):
                        i = dgi[0]
                        if i >= 64:
                            return
                        dgi[0] += 1
                        if i % 2 == 0:
                            P.add("act", lambda e, i=i: e.activation(out=dg[:, i, :], in_=i2b[:], func=AF.Identity, scale=pv(12 + i)),
                                  r=["i2", "pvec"], w=["dg"])
                        else:
                            P.add("dve", lambda e, i=i: e.tensor_scalar(out=dg[:, i, :], in0=i2f[:], scalar1=pv(12 + i), scalar2=None, op0=ALU.mult),
                                  r=["i2", "pvec"], w=["dg"])

                for cc in range(2):
                    for n in range(4):
                        b = nextbank()
                        proj_fm(ubb, cc * 128, n, b)
                        sg = sig[n % 2]
                        P.add("act", lambda e, b=b, sg=sg: e.activation(out=sg, in_=ps[:, b, :], func=AF.Sigmoid),
                              r=[PB(b)], w=[("sig", n % 2)])
                        b = nextbank()
                        proj_fm(ua, cc * 128, n, b)
                        P.add("dve", lambda e, b=b, sg=sg, cc=cc, n=n: e.tensor_tensor(out=gl[:, cc, 15 + n * 512:15 + (n + 1) * 512], in0=ps[:, b, :], in1=sg, op=ALU.mult),
                              r=[PB(b), ("sig", n % 2)], w=["gl"])
                        diag_some(8)
                    P.add("dve", lambda e, cc=cc: e.tensor_copy(out=gl0[0:64, cc, :], in_=gl[0:64, cc, :]), r=["gl", "glpad"], w=["gl0"])
                    P.add("dve", lambda e, cc=cc: e.tensor_copy(out=gl0[64:128, cc, 0:GLW - 1], in_=gl[0:64, cc, 1:GLW]), r=["gl", "glpad"], w=["gl0"])
                    P.add("dve", lambda e, cc=cc: e.tensor_copy(out=gl[0:64, cc, 0:GLW - 1], in_=gl[64:128, cc, 1:GLW]), r=["gl", "glpad", "gl0"], w=["gl"])
                for u in (ua, ubb):
                    done_unit(u)

                def conv_stage(n):
                    c32 = c32s[n % 2]
                    cb = [nextbank(), nextbank()]
                    for cc in range(2):
                        for j in range(16):
                            for a in range(2):
                                src = gl0 if a == 0 else gl
                                P.add("pe", lambda e, cc=cc, j=j, a=a, src=src, b=cb[cc]: e.matmul(
                                    ps[64 * a:64 * a + 64, b, :], lhsT=dg[:, (cc * 2 + a) * 16 + j, :], rhs=src[:, cc, n * 512 + 2 * j:n * 512 + 2 * j + 512],
                                    start=(j == 0), stop=(j == 15), skip_group_check=True, tile_position=(0, 64 * a)),
                                    r=["dg", "gl", "gl0", "glpad", "gl0pad"], w=[PB(cb[cc])])
                        P.add("dve", lambda e, cc=cc, b=cb[cc]: e.tensor_scalar(out=c32[:, cc, :], in0=ps[:, b, :], scalar1=pv(0 + cc), scalar2=None, op0=ALU.add),
                              r=[PB(cb[cc]), "pvec"], w=[("c32", n % 2, cc)])

                sbanks = {}

                def stats_a1(n):
                    c32 = c32s[n % 2]
                    for cc in range(2):
                        P.add("act", lambda e, cc=cc: e.activation(out=cbf[:, cc, :], in_=c32[:, cc, :], func=AF.Copy), r=[("c32", n % 2, cc)], w=[("cbf", cc)])
                        P.add("act", lambda e, cc=cc: e.activation(out=sqb[:, cc, :], in_=c32[:, cc, :], func=AF.Square), r=[("c32", n % 2, cc)], w=[("sqb", cc)])

                def stats_a2(n):
                    bm, be = nextbank(), nextbank()
                    sbanks[n] = (bm, be)
                    for cc in range(2):
                        P.add("pe", lambda e, cc=cc: e.matmul(ps[:, bm, :], lhsT=onesm[:], rhs=cbf[:, cc, :], start=(cc == 0), stop=(cc == 1)),
                              r=["onesm", ("cbf", cc)], w=[PB(bm)])
                    for cc in range(2):
                        P.add("pe", lambda e, cc=cc: e.matmul(ps[:, be, :], lhsT=onesm[:], rhs=sqb[:, cc, :], start=(cc == 0), stop=(cc == 1)),
                              r=["onesm", ("sqb", cc)], w=[PB(be)])

                def stats_b(n):
                    c32 = c32s[n % 2]
                    bm, be = sbanks[n]
                    P.add("act", lambda e: e.activation(out=msq, in_=ps[:, bm, :], func=AF.Square), r=[PB(bm)], w=["msq"])
                    P.add("dve", lambda e: e.tensor_tensor(out=rs, in0=ps[:, be, :], in1=msq, op=ALU.subtract), r=[PB(be), "msq"], w=["msq"])
                    P.add("act", lambda e: e.activation(out=rs, in_=rs, func=AF.Ln, bias=cst[:, 1:2], scale=1.0), r=["msq", "cst"], w=["msq"])
                    P.add("act", lambda e: e.activation(out=rs, in_=rs, func=AF.Exp, scale=-0.5), r=["msq"], w=["msq"])
                    for cc in range(2):
                        P.add("dve", lambda e, cc=cc: e.tensor_tensor(out=c32[:, cc, :], in0=c32[:, cc, :], in1=ps[:, bm, :], op=ALU.subtract), r=[("c32", n % 2, cc), PB(bm), "msq"], w=[("c32", n % 2, cc)])
                        P.add("dve", lambda e, cc=cc: e.tensor_tensor(out=c32[:, cc, :], in0=c32[:, cc, :], in1=rs, op=ALU.mult), r=[("c32", n % 2, cc), "msq"], w=[("c32", n % 2, cc)])
                        P.add("act", lambda e, cc=cc: e.activation(out=hsb[:, cc, :], in_=c32[:, cc, :], func=AF.Silu, scale=pv(2 + cc), bias=pv(4 + cc)),
                              r=[("c32", n % 2, cc), "pvec"], w=[("hsb", cc)])

                def pw_stage(n):
                    for fo in range(2):
                        b2 = nextbank()
                        proj_fm(ugb, fo * 128, n, b2)
                        sg = sig[fo]
                        P.add("act", lambda e, b2=b2, sg=sg: e.activation(out=sg, in_=ps[:, b2, :], func=AF.Silu),
                              r=[PB(b2)], w=[("sig", fo)])
                        b = nextbank()
                        for cc in range(2):
                            P.add("pe", lambda e, cc=cc, fo=fo, b=b: e.matmul(ps[:, b, :], lhsT=PWs[:, cc, fo * 128:(fo + 1) * 128], rhs=hsb[:, cc, :], start=(cc == 0), stop=(cc == 1)),
                                  r=["PWs", ("hsb", cc)], w=[PB(b)])
                        P.add("dve", lambda e, fo=fo, b=b, sg=sg: e.tensor_tensor(out=yT[:, 2 + fo, n * 512:(n + 1) * 512], in0=ps[:, b, :], in1=sg, op=ALU.mult),
                              r=[PB(b), ("sig", fo)], w=["yT"])

                conv_stage(0)
                stats_a1(0)
                stats_a2(0)
                stats_b(0)
                for n in range(4):
                    if n + 1 < 4:
                        conv_stage(n + 1)
                        stats_a1(n + 1)
                    pw_stage(n)
                    if n + 1 < 4:
                        stats_a2(n + 1)
                        stats_b(n + 1)
                done_unit(ugb)
                P.barrier(scratch)

                up, ugc = ub + 7, ub + 8
                AR.reset()
                XW = S + 16
                gTc = AR.alloc(2 * S, BF16).rearrange("p (c n) -> p c n", c=2)
                PWt = AR.alloc(2 * 128, BF16).rearrange("p (c n) -> p c n", c=2)
                Xs = [AR.alloc(XW) for _ in range(2)]
                A_ = AR.alloc(XW)
                B_ = AR.alloc(XW)
                plb1 = AR.alloc(S, BF16)
                plbs = [plb1, Xs[0][:, 0:S // 2].bitcast(BF16)]
                P.add("pool", lambda e: e.memset(PWt, 0.0), w=["PWt"])
                for cc in range(2):
                    for hf in range(2):
                        P.add("pool", lambda e, cc=cc, hf=hf: e.dma_start(out=PWt[64 * hf:64 * hf + 64, cc, 64 * hf:64 * hf + 64], in_=poolw_d[l, 2 * cc + hf]),
                              r=["PWt"], w=["PWt"], dma="pwt")
                for cc in range(2):
                    P.add("pool", lambda e, cc=cc: e.memset(Xs[cc][:, 0:8], 0.0), w=[("Xpad", cc)])
                    P.add("pool", lambda e, cc=cc: e.memset(Xs[cc][:, 8 + S:XW], 0.0), w=[("Xpad", cc)])
                for cc in range(2):
                    for n in range(4):
                        b = nextbank()
                        proj_fm(up, cc * 128, n, b)
                        P.add("act", lambda e, b=b, n=n, cc=cc: e.activation(out=Xs[cc][:, 8 + n * 512:8 + (n + 1) * 512], in_=ps[:, b, :], func=AF.Copy),
                              r=[PB(b)], w=[("X", cc)])
                for cc in range(2):
                    for n in range(4):
                        b = nextbank()
                        proj_fm(ugc, cc * 128, n, b)
                        P.add("act", lambda e, b=b, cc=cc, n=n: e.activation(out=gTc[:, cc, n * 512:(n + 1) * 512], in_=ps[:, b, :], func=AF.Silu),
                              r=[PB(b)], w=["gTc"])
                done_unit(up)
                done_unit(ugc)
                def c_sums(cc):
                    X = Xs[cc]
                    add = lambda o, a, b_, tok_r, tok_w: P.add("dve", lambda e: e.tensor_tensor(out=o, in0=a, in1=b_, op=ALU.add), r=tok_r, w=tok_w)
                    xt = [("X", cc), ("Xpad", cc)]
                    add(A_[:, 1:XW], X[:, 0:XW - 1], X[:, 1:XW], xt, ["A"])
                    if cc == 0:
                        add(B_[64:128, 2:XW - 1], A_[64:128, 1:XW - 2], A_[64:128, 3:XW], ["A"], ["B"])
                    else:
                        add(B_[:, 2:XW - 1], A_[:, 1:XW - 2], A_[:, 3:XW], ["A"], ["B"])
                        add(A_[:, 4:XW - 3], B_[:, 2:XW - 5], B_[:, 6:XW - 1], ["B"], ["A"])
                        add(B_[64:128, 8:XW - 8], A_[64:128, 4:XW - 12], A_[64:128, 12:XW - 4], ["A"], ["B"])

                def c_pool(cc):
                    X = Xs[cc]
                    plb = plbs[cc]
                    srcs = [(A_, 2.0), (B_, 4.0)] if cc == 0 else [(A_, 8.0), (B_, 16.0)]
                    for hf in range(2):
                        src, wv = srcs[hf]
                        pr = slice(64 * hf, 64 * hf + 64)
                        for (e0, tcol) in ((8, 0), (8 + S - 8, 8)):
                            dc = DEPTH * NVL + cc * 16 + tcol
                            P.add("dve", lambda e, src=src, pr=pr, e0=e0, dc=dc: e.tensor_tensor(out=src[pr, e0:e0 + 8], in0=src[pr, e0:e0 + 8], in1=pvec[pr, dc:dc + 8], op=ALU.mult),
                                  r=["A", "B", "pvec"], w=["A", "B"])
                        P.add("dve", lambda e, src=src, pr=pr, wv=wv: e.scalar_tensor_tensor(out=plb[pr, :], in0=src[pr, 8:8 + S], scalar=1.0 / wv, in1=X[pr, 8:8 + S],
                                                                                             op0=ALU.mult, op1=ALU.subtract),
                              r=["A", "B", ("X", cc)], w=[("plb", cc)] + ([("X", 0)] if cc == 1 else []))

                def c_out(cc):
                    plb = plbs[cc]
                    for n in range(4):
                        b = nextbank()
                        P.add("pe", lambda e, n=n, b=b: e.matmul(ps[:, b, :], lhsT=PWt[:, cc, :], rhs=plb[:, n * 512:(n + 1) * 512], start=True, stop=True),
                              r=["PWt", ("plb", cc)], w=[PB(b)])
                        P.add("dve", lambda e, n=n, b=b: e.scalar_tensor_tensor(out=yT[:, 4 + cc, n * 512:(n + 1) * 512], in0=ps[:, b, :], scalar=pv(6 + cc),
                                                                              in1=gTc[:, cc, n * 512:(n + 1) * 512], op0=ALU.mult, op1=ALU.mult),
                              r=[PB(b), "pvec", "gTc"], w=["yT"])

                c_sums(0)
                c_pool(0)
                c_sums(1)
                c_pool(1)
                c_out(0)
                c_out(1)
                P.barrier(scratch)

                uu, uvv, ugd = ub + 9, ub + 10, ub + 11
                AR.reset()
                guT = AR.alloc(2 * S, BF16).rearrange("p (c n) -> p c n", c=2)
                gv = AR.alloc(NT * 256).rearrange("p (t n) -> p t n", t=NT)
                vn = AR.alloc(NT * 256, BF16).rearrange("p (t n) -> p t n", t=NT)
                wsT = AR.alloc(4 * 128, BF16).rearrange("p (h n) -> p h n", h=4)
                bsb = AR.alloc(2 * 128).rearrange("p (c n) -> p c n", c=2)
                bias2 = AR.alloc(2 * 512).rearrange("p (c q n) -> p c q n", c=2, q=4)
                gsm = [AR.alloc(512, BF16) for _ in range(2)]
                s1 = AR.alloc(16)
                s2 = AR.alloc(16)
                mu = AR.alloc(16)
                rsd = AR.alloc(16)
                jk = AR.alloc(256)
                tmx = [AR.alloc(512) for _ in range(2)]
                P.add("pool", lambda e: e.dma_start(out=wsT, in_=swt_d[l]), r=["BAR"], w=["wsT"], dma="wsT")
                P.add("sp", lambda e: e.dma_start(out=bsb, in_=sbb_d[l]), r=["BAR"], w=["bsb"], dma="bsb")
                for cc in range(2):
                    for n in range(4):
                        b = nextbank()
                        proj_fm(uu, cc * 128, n, b)
                        P.add("act", lambda e, b=b, cc=cc, n=n: e.activation(out=guT[:, cc, n * 512:(n + 1) * 512], in_=ps[:, b, :], func=AF.Gelu_apprx_tanh),
                              r=[PB(b)], w=["guT"])
                for t2 in range(8):
                    b = nextbank()
                    for tt in range(2):
                        proj_tm(uvv, 0, 256, 2 * t2 + tt, b, tt * 256)
                    for tt in range(2):
                        t = 2 * t2 + tt
                        P.add("act", lambda e, b=b, t=t, tt=tt: e.activation(out=gv[:, t, :], in_=ps[:, b, tt * 256:(tt + 1) * 256], func=AF.Gelu_apprx_tanh, accum_out=s1[:, t:t + 1]),
                              r=[PB(b)], w=[("gv", t), "s1"])
                    for tt in range(2):
                        t = 2 * t2 + tt
                        P.add("dve", lambda e, t=t: e.scalar_tensor_tensor(out=jk, in0=gv[:, t, :], scalar=1.0, in1=gv[:, t, :], op0=ALU.mult, op1=ALU.mult,
                                                                          accum_out=s2[:, t:t + 1]),
                              r=[("gv", t)], w=["s2", "jk"])
                for u in (uu, uvv):
                    done_unit(u)
                P.add("dve", lambda e: e.tensor_scalar(out=mu, in0=s1, scalar1=1.0 / 256.0, scalar2=None, op0=ALU.mult), r=["s1"], w=["mu"])
                P.add("dve", lambda e: e.tensor_tensor(out=rsd, in0=mu, in1=mu, op=ALU.mult), r=["mu"], w=["rsd"])
                P.add("dve", lambda e: e.scalar_tensor_tensor(out=rsd, in0=s2, scalar=1.0 / 256.0, in1=rsd, op0=ALU.mult, op1=ALU.subtract), r=["s2", "rsd"], w=["rsd"])
                P.add("act", lambda e: e.activation(out=rsd, in_=rsd, func=AF.Sqrt, bias=cst[:, 1:2], scale=1.0), r=["rsd", "cst"], w=["rsd"])
                P.add("dve", lambda e: e.reciprocal(out=rsd, in_=rsd), r=["rsd"], w=["rsd"])
                for t in range(NT):
                    P.add("dve", lambda e, t=t: e.tensor_scalar(out=vn[:, t, :], in0=gv[:, t, :], scalar1=mu[:, t:t + 1], scalar2=rsd[:, t:t + 1], op0=ALU.subtract, op1=ALU.mult),
                          r=[("gv", t), "mu", "rsd"], w=[("vn", t)])
                bw = nextbank()
                for hd in range(4):
                    P.add("pe", lambda e, hd=hd, bw=bw: e.matmul(ps[64 * (hd % 2):64 * (hd % 2) + 64, bw, (hd // 2) * 128:(hd // 2) * 128 + 128], lhsT=ones_bf[:, 0:64], rhs=wsT[:, hd, :],
                                                                 start=True, stop=True, skip_group_check=True, tile_position=(0, 64 * (hd % 2))),
                          r=["ones", "wsT", "BAR"], w=[PB(bw)])
                for cc in range(2):
                    for q in range(4):
                        P.add("dve", lambda e, cc=cc, bw=bw, q=q: e.scalar_tensor_tensor(out=bias2[:, cc, q, :], in0=ps[:, bw, cc * 128:(cc + 1) * 128], scalar=pv(10 + cc), in1=bsb[:, cc, :],
                                                                                        op0=ALU.mult, op1=ALU.add),
                              r=[PB(bw), "pvec", "bsb"], w=["bias2"])
                for cc in range(2):
                    for n in range(4):
                        b = nextbank()
                        for q in range(4):
                            tq = 4 * n + q
                            for h2 in range(2):
                                hd = 2 * cc + h2
                                P.add("pe", lambda e, b=b, q=q, tq=tq, h2=h2, hd=hd: e.matmul(ps[64 * h2:64 * h2 + 64, b, q * 128:(q + 1) * 128], lhsT=vn[:, tq, hd * 64:(hd + 1) * 64], rhs=wsT[:, hd, :],
                                                                                             start=True, stop=True, skip_group_check=True, tile_position=(0, 64 * h2)),
                                      r=[("vn", tq), "wsT"], w=[PB(b)])
                        tm = tmx[n % 2]
                        P.add("dve", lambda e, b=b, cc=cc, tm=tm: e.scalar_tensor_tensor(out=tm, in0=ps[:, b, :], scalar=pv(8 + cc),
                                                                                        in1=bias2[:, cc].rearrange("p q n -> p (q n)"), op0=ALU.mult, op1=ALU.add),
                              r=[PB(b), "pvec", "bias2"], w=[("tm", n % 2)])
                        P.add("dve", lambda e, cc=cc, n=n, tm=tm: e.tensor_tensor(out=tm, in0=tm, in1=guT[:, cc, n * 512:(n + 1) * 512], op=ALU.mult),
                              r=[("tm", n % 2), "guT"], w=[("tm", n % 2)])
                        b2 = nextbank()
                        proj_fm(ugd, cc * 128, n, b2)
                        gs = gsm[n % 2]
                        P.add("act", lambda e, b2=b2, gs=gs: e.activation(out=gs, in_=ps[:, b2, :], func=AF.Silu),
                              r=[PB(b2)], w=[("gsm", n % 2)])
                        P.add("pool", lambda e, cc=cc, n=n, tm=tm, gs=gs: e.tensor_tensor(out=yT[:, 6 + cc, n * 512:(n + 1) * 512], in0=tm, in1=gs, op=ALU.mult),
                              r=[("tm", n % 2), ("gsm", n % 2)], w=["yT"])
                done_unit(ugd)
                if dbg and li == 0:
                    P.add("sp", lambda e: e.dma_start(out=dbg_y, in_=yT[:]), r=["yT"], dma="dbg")
                P.barrier(scratch)

                uo = ub + 12
                AR.reset()
                last = (li == depth - 1)
                if not last:
                    nrm = norm_setup(normg_d[l + 1:l + 2, :])
                elif final_norm:
                    nrm = norm_setup(fg_d)
                for t in range(NT):
                    for half in range(2):
                        b = 4 + (2 * t + half) % 4
                        slot = (uo + 2 * half) % NSLOT
                        wv = wo_view(slot)
                        for yc in range(8):
                            P.add("pe", lambda e, b=b, wv=wv, yc=yc: e.matmul(ps[:, b, :], lhsT=yT[:, yc, t * 128:(t + 1) * 128], rhs=wv[:, yc, :],
                                                                             start=(yc == 0), stop=(yc == 7)),
                                  r=["yT", ("W", slot), ("W", slot + 1)], w=[PB(b)])
                        P.add("dve", lambda e, b=b, half=half: e.tensor_tensor(out=xres[:, t, half * 512:(half + 1) * 512], in0=ps[:, b, :], in1=xres[:, t, half * 512:(half + 1) * 512], op=ALU.add),
                              r=[PB(b), ("x", t)], w=[("x", t)])
                    if not last:
                        norm_stats_act(nrm, t)
                        if t >= 1:
                            norm_stats_dve(nrm, t - 1)
                            normA_h(nrm, t - 1)
                        if t % 4 == 1 and t >= 5:
                            for t2 in range(t - 5, t - 1):
                                normA_T(nrm, t2)
                    elif final_norm:
                        norm_stats(nrm, t)
                        if t >= 1:
                            final_tile(nrm, t - 1)
                if last and final_norm:
                    final_tile(nrm, NT - 1)
                if not last:
                    norm_stats_dve(nrm, NT - 1)
                    normA_h(nrm, NT - 1)
                    for t2 in range(NT - 4, NT):
                        normA_T(nrm, t2)
                for q in range(4):
                    done_unit(uo + q)

            if not final_norm:
                for t in range(NT):
                    P.add("sp", lambda e, t=t: e.dma_start(out=out_d[t * 128:(t + 1) * 128, :], in_=xres[:, t, :]), r=[("x", t)], dma=f"o{t % 2}")
        _run()
    return nc


def _na_tables(rpb):
    p = np.arange(128)
    hlf, kc = p // 64, p % 64
    tab = np.zeros((DEPTH, 8, 128, TABW), np.float32)
    mask = np.full((128, TABW), NEG, np.float32)

    def fill(col0, nrows, kr_of_half, r_of_ri, need_edge_rule):
        q = np.arange(nrows * 64)
        ri, qc = q // 64, q % 64
        cs = np.clip(qc - 8, 0, 48)
        col_ok = (kc[:, None] >= cs[None, :]) & (kc[:, None] < cs[None, :] + 16)
        dc = np.clip(kc[:, None] - qc[None, :], -15, 15) + 15
        kr = kr_of_half(hlf)[:, None]
        r = r_of_ri(ri)[None, :]
        valid = col_ok & np.ones_like(kr + r, bool)
        if need_edge_rule:
            rs = np.clip(r - 4, 0, 24)
            inwin = (r >= kr - 3) & (r <= kr + 4)
            valid = valid & (kr >= rs) & (kr <= rs + 7) & (~inwin)
        drr = np.clip(kr - r, -7, 7) + 7
        g = rpb[:, :, drr, dc]
        tab[:, :, :, col0:col0 + nrows * 64] = np.where(valid[None, None], g, 0.0)
        mask[:, col0:col0 + nrows * 64] = np.where(valid, 0.0, NEG)

    q = np.arange(512)
    ri, qc = q // 64, q % 64
    cs = np.clip(qc - 8, 0, 48)
    col_ok = (kc[:, None] >= cs[None, :]) & (kc[:, None] < cs[None, :] + 16)
    dc = np.clip(kc[:, None] - qc[None, :], -15, 15) + 15
    drr = np.broadcast_to((3 - ri)[None, :] + 7, (128, 512))
    g = rpb[:, :, drr, dc]
    tab[:, :, :, 0:512] = np.where(col_ok[None, None], g, 0.0)
    mask[:, 0:512] = np.where(col_ok, 0.0, NEG)
    for j, col0 in ((2, 512), (3, 768), (12, 1024), (13, 1280)):
        base_r = 0 if j < 8 else 28
        fill(col0, 4, lambda h, j=j: 2 * j + h, lambda ri, base_r=base_r: base_r + ri, True)
    return tab, mask


def _pvec(conv_dw_w, conv_dw_b, conv_ln_g, conv_ln_b, pool_scale, sgu_ln_g, sgu_ln_b):
    pv = np.zeros((128, DEPTH * NVL + 34), np.float32)
    pv[:, DEPTH * NVL + 32] = ((np.arange(128) // 32) % 2 == 0)
    pv[:, DEPTH * NVL + 33] = ((np.arange(128) // 32) % 2 == 1)
    col = lambda v: v.reshape(2, 128).T
    for l in range(DEPTH):
        b = l * NVL
        pv[:, b + 0:b + 2] = col(conv_dw_b[l])
        pv[:, b + 2:b + 4] = col(conv_ln_g[l])
        pv[:, b + 4:b + 6] = col(conv_ln_b[l])
        pv[:, b + 6:b + 8] = col(pool_scale[l])
        pv[:, b + 8:b + 10] = col(sgu_ln_g[l])
        pv[:, b + 10:b + 12] = col(sgu_ln_b[l])
        w = conv_dw_w[l]
        wz = np.concatenate([w, np.zeros((1, 256), np.float32)], 0)
        for cc in range(2):
            for a in range(2):
                ch = cc * 128 + 64 * a + np.arange(64)
                for j in range(16):
                    v = np.zeros(128, np.float32)
                    first, second = wz[2 * j, ch], wz[2 * j + 1, ch]
                    if a == 0:
                        v[0:64], v[64:128] = first, second
                    else:
                        v[64:128], v[0:64] = first, second
                    pv[:, b + 12 + (cc * 2 + a) * 16 + j] = v
    for cc in range(2):
        for p in range(128):
            w = POOL_W[2 * cc + p // 64]
            for i, t in enumerate(list(range(8)) + list(range(S - 8, S))):
                lo = min(max(t - w // 2, 0), S)
                hi = min(max(t - w // 2 + w, 0), S)
                pv[p, DEPTH * NVL + cc * 16 + i] = w / (hi - lo)
    return pv


_NC_CACHE = {}


def _get_nc(key, **kw):
    if key not in _NC_CACHE:
        _NC_CACHE[key] = build(**kw)
    return _NC_CACHE[key]


def _host_inputs(x, norm_g, w_in, na_rpb, conv_dw_w, conv_dw_b, conv_ln_g, conv_ln_b, conv_pw,
                 pool_w, pool_scale, sgu_ln_g, sgu_ln_b, sgu_w, sgu_b, w_out, final_g):
    f = lambda a: np.ascontiguousarray(np.asarray(a, dtype=np.float32))
    tab, mask = _na_tables(f(na_rpb))
    sgu_wT = np.ascontiguousarray(f(sgu_w).transpose(0, 3, 1, 2))
    sb = f(sgu_b)
    sgu_bb = np.ascontiguousarray(np.repeat(sb.reshape(DEPTH, 2, 2, 1, 128), 64, axis=3)
                                  .reshape(DEPTH, 2, 128, 128).transpose(0, 2, 1, 3))
    shared = {
        "norm_g": f(norm_g), "w_in": f(w_in), "w_out": f(w_out), "final_g": f(final_g).reshape(1, D),
        "na_tab": tab, "na_mask": mask,
        "pvec": _pvec(f(conv_dw_w), f(conv_dw_b), f(conv_ln_g), f(conv_ln_b), f(pool_scale), f(sgu_ln_g), f(sgu_ln_b)),
        "conv_pw": f(conv_pw), "pool_w": f(pool_w), "sgu_wT": sgu_wT, "sgu_bb": sgu_bb,
        "ident": np.eye(128, dtype=np.float32).astype(ml_dtypes.bfloat16),
    }
    return shared


def kernel(**inputs):
    x = np.ascontiguousarray(np.asarray(inputs["x"], dtype=np.float32))
    shared = _host_inputs(**inputs)
    nc = _get_nc("full", depth=DEPTH, first_layer=0, final_norm=True)
    in_maps = [dict(shared, x=x[b]) for b in range(NCORES)]
    res = run_bass_kernel_spmd(nc, in_maps, core_ids=list(range(NCORES)))
    return np.stack([np.asarray(r["out"], dtype=np.float32) for r in res.results], axis=0)
```
